# Optimizing a Trainium2 kernel written in Bass

```python
import numpy as np
import jax, jax.numpy as jnp
from jax import lax

D_MODEL = 1024
BATCH = 8
SEQ = 2048
DEPTH = 2

N_A_LAYERS = DEPTH // 2
N_B_LAYERS = DEPTH - N_A_LAYERS
D_FF = 2816
CONV_WIDTH = 3
N_HEADS = 16
HEAD_DIM = D_MODEL // N_HEADS
N_KV_GROUPS = 4
HEADS_PER_GROUP = N_HEADS // N_KV_GROUPS
CMP_BLOCK = 32
CMP_STRIDE = 16
CMP_HIDDEN = 4 * HEAD_DIM
SEL_BLOCK = 64
N_SELECT = 16
WINDOW = 512
N_BRANCH = 3
N_KV_SLOTS = 6
QUERY_CHUNK = 32
EPS = 1e-6
FORCE_SCORE = 1e9
MASK_SCORE = -1e30

kernel_name = "yoco_shortconv_nsa_macaron"


def _rmsnorm(x, g):
    xf = x.astype(jnp.float32)
    y = xf * lax.rsqrt(jnp.mean(xf * xf, axis=-1, keepdims=True) + EPS)
    return (y * g.astype(jnp.float32)).astype(x.dtype)


def _swiglu(x, w_gate_up, w_down):
    a, b = jnp.split(x @ w_gate_up, 2, axis=-1)
    return (jax.nn.silu(a) * b) @ w_down


def _short_conv(x, w_in, conv_w, w_out):
    b_gate, c_gate, u = jnp.split(x @ w_in, 3, axis=-1)
    v = c_gate * u
    conv = lax.conv_general_dilated(
        v, conv_w[:, None, :], window_strides=(1,), padding=[(CONV_WIDTH - 1, 0)],
        dimension_numbers=("NWC", "WIO", "NWC"), feature_group_count=D_MODEL)
    return (b_gate * conv) @ w_out


def _compress(x_raw, pos, w1, b1, w2):
    s = x_raw.shape[2]
    n_cmp = (s - CMP_BLOCK) // CMP_STRIDE + 1
    idx = np.arange(n_cmp)[:, None] * CMP_STRIDE + np.arange(CMP_BLOCK)[None, :]
    blocks = x_raw[:, :, idx] + pos
    flat = blocks.reshape(blocks.shape[:3] + (CMP_BLOCK * HEAD_DIM,))
    return jax.nn.gelu(flat @ w1 + b1) @ w2


def _shared_kv(h, kv_norm, kv_w, cmp_pos, cmp_w1, cmp_b1, cmp_w2, k_norm):
    b, s, _ = h.shape
    kv = (_rmsnorm(h, kv_norm) @ kv_w).reshape(b, s, N_KV_SLOTS, N_KV_GROUPS, HEAD_DIM)
    kv = jnp.transpose(kv, (2, 0, 3, 1, 4))
    k_cmp = _rmsnorm(_compress(kv[0], cmp_pos[0], cmp_w1[0], cmp_b1[0], cmp_w2[0]), k_norm[0])
    v_cmp = _compress(kv[1], cmp_pos[1], cmp_w1[1], cmp_b1[1], cmp_w2[1])
    n_blk = s // SEL_BLOCK
    k_sel = _rmsnorm(kv[2], k_norm[1]).reshape(b, N_KV_GROUPS, n_blk, SEL_BLOCK, HEAD_DIM)
    v_sel = kv[3].reshape(b, N_KV_GROUPS, n_blk, SEL_BLOCK, HEAD_DIM)
    pad = ((0, 0), (0, 0), (WINDOW, 0), (0, 0))
    k_win = jnp.pad(_rmsnorm(kv[4], k_norm[2]), pad)
    v_win = jnp.pad(kv[5], pad)
    return (k_cmp, v_cmp, k_sel, v_sel, k_win, v_win)


def _softmax_f32(logits, mask):
    return jax.nn.softmax(jnp.where(mask, logits, MASK_SCORE), axis=-1)


def _nsa(x, w_qg, q_norm, w_o, k_cmp, v_cmp, k_sel, v_sel, k_win, v_win):
    b, s, _ = x.shape
    n_chunk = s // QUERY_CHUNK
    n_cmp = k_cmp.shape[2]
    n_blk = k_sel.shape[2]
    n_sel = min(N_SELECT, n_blk)
    scale = HEAD_DIM ** -0.5

    proj = x @ w_qg
    q = _rmsnorm(proj[..., :N_HEADS * HEAD_DIM].reshape(b, s, N_HEADS, HEAD_DIM), q_norm)
    gate = jax.nn.sigmoid(proj[..., N_HEADS * HEAD_DIM:].astype(jnp.float32))
    q = q.reshape(b, n_chunk, QUERY_CHUNK, N_KV_GROUPS, HEADS_PER_GROUP, HEAD_DIM)
    q = q.transpose(1, 0, 3, 4, 2, 5)
    gate = gate.reshape(b, n_chunk, QUERY_CHUNK, N_KV_GROUPS, HEADS_PER_GROUP, N_BRANCH)
    gate = gate.transpose(1, 0, 3, 4, 2, 5)

    cmp_start = np.arange(n_cmp) * CMP_STRIDE
    cmp_end = cmp_start + CMP_BLOCK - 1
    blk_start = np.arange(n_blk) * SEL_BLOCK
    overlap = ((cmp_start[:, None] < blk_start[None, :] + SEL_BLOCK)
               & (cmp_start[:, None] + CMP_BLOCK > blk_start[None, :])).astype(np.float32)
    gather_blocks = jax.vmap(jax.vmap(lambda kb, ib: kb[ib]))
    blk_ids = jnp.arange(n_blk)

    def chunk_fn(args):
        c, qc, gc = args
        t = c * QUERY_CHUNK + jnp.arange(QUERY_CHUNK)
        logits = jnp.einsum("bghtd,bgnd->bghtn", qc, k_cmp).astype(jnp.float32) * scale
        valid = cmp_end[None, :] <= t[:, None]
        p_cmp = _softmax_f32(logits, valid) * valid
        o_cmp = jnp.einsum("bghtn,bgnd->bghtd", p_cmp.astype(v_cmp.dtype), v_cmp)
        imp = jnp.einsum("bghtn,nj->bgtj", p_cmp, overlap)
        cur = (t // SEL_BLOCK)[:, None]
        forced = (blk_ids[None] == 0) | (blk_ids[None] == cur) | (blk_ids[None] == cur - 1)
        imp = jnp.where(forced, FORCE_SCORE, imp)
        imp = jnp.where(blk_ids[None] > cur, MASK_SCORE, imp)
        _, idx = lax.top_k(imp, n_sel)
        ks = gather_blocks(k_sel, idx).reshape(b, N_KV_GROUPS, QUERY_CHUNK, n_sel * SEL_BLOCK, HEAD_DIM)
        vs = gather_blocks(v_sel, idx).reshape(b, N_KV_GROUPS, QUERY_CHUNK, n_sel * SEL_BLOCK, HEAD_DIM)
        kpos = (idx[..., None] * SEL_BLOCK + jnp.arange(SEL_BLOCK)).reshape(
            b, N_KV_GROUPS, QUERY_CHUNK, n_sel * SEL_BLOCK)
        smask = (kpos <= t[:, None])[:, :, None]
        logits = jnp.einsum("bghtd,bgtkd->bghtk", qc, ks).astype(jnp.float32) * scale
        p_sel = _softmax_f32(logits, smask)
        o_sel = jnp.einsum("bghtk,bgtkd->bghtd", p_sel.astype(vs.dtype), vs)
        start = c * QUERY_CHUNK
        kw = lax.dynamic_slice_in_dim(k_win, start, WINDOW + QUERY_CHUNK, axis=2)
        vw = lax.dynamic_slice_in_dim(v_win, start, WINDOW + QUERY_CHUNK, axis=2)
        wpos = start - WINDOW + jnp.arange(WINDOW + QUERY_CHUNK)
        wmask = ((wpos[None] <= t[:, None]) & (wpos[None] > t[:, None] - WINDOW)
                 & (wpos[None] >= 0))
        logits = jnp.einsum("bghtd,bgkd->bghtk", qc, kw).astype(jnp.float32) * scale
        p_win = _softmax_f32(logits, wmask)
        o_win = jnp.einsum("bghtk,bgkd->bghtd", p_win.astype(vw.dtype), vw)
        out = gc[..., 0:1] * o_cmp + gc[..., 1:2] * o_sel + gc[..., 2:3] * o_win
        return out.astype(qc.dtype)

    outs = lax.map(chunk_fn, (jnp.arange(n_chunk), q, gate))
    o = outs.transpose(1, 0, 4, 2, 3, 5).reshape(b, s, N_HEADS * HEAD_DIM)
    return o @ w_o


def setup_inputs(seed: int = 0) -> dict:
    key = jax.random.key(seed)
    ks = jax.random.split(key, 20)
    nrm = lambda k, shape, scale: jax.random.normal(k, shape, jnp.float32) * scale
    gain = lambda k, shape: 1.0 + 0.02 * jax.random.normal(k, shape, jnp.float32)
    qg_width = N_HEADS * HEAD_DIM + N_BRANCH * N_HEADS
    return {
        "x": nrm(ks[0], (BATCH, SEQ, D_MODEL), 1.0),
        "ffn_norm": gain(ks[1], (DEPTH, 2, D_MODEL)),
        "ffn_w_gate_up": nrm(ks[2], (DEPTH, 2, D_MODEL, 2 * D_FF), D_MODEL ** -0.5),
        "ffn_w_down": nrm(ks[3], (DEPTH, 2, D_FF, D_MODEL), D_FF ** -0.5),
        "mix_norm": gain(ks[4], (DEPTH, D_MODEL)),
        "conv_w_in": nrm(ks[5], (N_A_LAYERS, D_MODEL, 3 * D_MODEL), D_MODEL ** -0.5),
        "conv_w": nrm(ks[6], (N_A_LAYERS, CONV_WIDTH, D_MODEL), CONV_WIDTH ** -0.5),
        "conv_w_out": nrm(ks[7], (N_A_LAYERS, D_MODEL, D_MODEL), D_MODEL ** -0.5),
        "kv_norm": gain(ks[8], (D_MODEL,)),
        "kv_w": nrm(ks[9], (D_MODEL, N_KV_SLOTS * N_KV_GROUPS * HEAD_DIM), D_MODEL ** -0.5),
        "cmp_pos": nrm(ks[10], (2, CMP_BLOCK, HEAD_DIM), 0.1),
        "cmp_w1": nrm(ks[11], (2, CMP_BLOCK * HEAD_DIM, CMP_HIDDEN), (CMP_BLOCK * HEAD_DIM) ** -0.5),
        "cmp_b1": nrm(ks[12], (2, CMP_HIDDEN), 0.01),
        "cmp_w2": nrm(ks[13], (2, CMP_HIDDEN, HEAD_DIM), CMP_HIDDEN ** -0.5),
        "k_norm": gain(ks[14], (N_BRANCH, HEAD_DIM)),
        "nsa_w_qg": nrm(ks[15], (N_B_LAYERS, D_MODEL, qg_width), D_MODEL ** -0.5),
        "q_norm": gain(ks[16], (N_B_LAYERS, HEAD_DIM)),
        "nsa_w_o": nrm(ks[17], (N_B_LAYERS, N_HEADS * HEAD_DIM, D_MODEL), (N_HEADS * HEAD_DIM) ** -0.5),
    }


def reference(x, ffn_norm, ffn_w_gate_up, ffn_w_down, mix_norm, conv_w_in, conv_w, conv_w_out,
              kv_norm, kv_w, cmp_pos, cmp_w1, cmp_b1, cmp_w2, k_norm, nsa_w_qg, q_norm, nsa_w_o):
    h = x
    shared = None
    for layer in range(DEPTH):
        h = h + 0.5 * _swiglu(_rmsnorm(h, ffn_norm[layer, 0]), ffn_w_gate_up[layer, 0], ffn_w_down[layer, 0])
        hn = _rmsnorm(h, mix_norm[layer])
        if layer < N_A_LAYERS:
            h = h + _short_conv(hn, conv_w_in[layer], conv_w[layer], conv_w_out[layer])
        else:
            i = layer - N_A_LAYERS
            h = h + _nsa(hn, nsa_w_qg[i], q_norm[i], nsa_w_o[i], *shared)
        h = h + 0.5 * _swiglu(_rmsnorm(h, ffn_norm[layer, 1]), ffn_w_gate_up[layer, 1], ffn_w_down[layer, 1])
        if layer == N_A_LAYERS - 1:
            shared = _shared_kv(h, kv_norm, kv_w, cmp_pos, cmp_w1, cmp_b1, cmp_w2, k_norm)
    return h
```

```python
from contextlib import ExitStack

import numpy as np
import concourse.bass as bass
import concourse.mybir as mybir
from concourse.bass_utils import run_bass_kernel_spmd

F32 = mybir.dt.float32
BF16 = mybir.dt.bfloat16
AF = mybir.ActivationFunctionType
ALU = mybir.AluOpType
AX = mybir.AxisListType


class Node:
    __slots__ = ("eng", "fn", "deps", "signal", "sem", "val", "dma", "seq")

    def __init__(self, eng, fn, dma):
        self.eng = eng
        self.fn = fn
        self.dma = dma
        self.deps = []
        self.signal = False
        self.sem = None
        self.val = 0
        self.seq = 0


class Prog:
    ENGS = ("pe", "act", "dve", "pool", "sp")
    N_DMA_SEMS = 16

    def __init__(self, nc):
        self.nc = nc
        self.nodes = {e: [] for e in self.ENGS}
        self.last_w = {}
        self.readers = {}
        self.dma_nodes = {"pool": [], "sp": [], "act": []}
        self.fence = []
        self.fenced = set()
        self.out_dmas = []

    def new_phase(self):
        self.fence = [self.nodes[e][-1] for e in self.ENGS if self.nodes[e]]
        self.fenced = set()

    def op(self, eng, fn, reads=(), writes=(), dma=False, arena=True, out=False):
        n = Node(eng, fn, dma)
        deps = []
        for k in reads:
            w = self.last_w.get(k)
            if w is not None:
                deps.append(w)
        for k in writes:
            w = self.last_w.get(k)
            if w is not None:
                deps.append(w)
            r = self.readers.get(k)
            if r:
                deps.extend(r.values())
        if arena and eng not in self.fenced:
            deps.extend(self.fence)
            self.fenced.add(eng)
        if dma:
            lst = self.dma_nodes[eng]
            i = len(lst)
            if i >= self.N_DMA_SEMS:
                deps.append(lst[i - self.N_DMA_SEMS])
            lst.append(n)
            if out:
                self.out_dmas.append(n)
        for k in reads:
            rk = id(n) if dma else eng
            self.readers.setdefault(k, {})[rk] = n
        for k in writes:
            self.last_w[k] = n
            self.readers[k] = {}
        seen = set()
        for d in deps:
            if d is n or id(d) in seen:
                continue
            seen.add(id(d))
            if d.eng == "pe" and eng == "pe" and not d.dma and not dma:
                continue
            n.deps.append(d)
            d.signal = True
        n.seq = len(self.nodes[eng])
        self.nodes[eng].append(n)
        return n

    def emit(self, block, sems, dma_sems):
        nc = self.nc
        for e in self.ENGS:
            c = 0
            for n in self.nodes[e]:
                if n.dma:
                    continue
                if n.signal:
                    c += 1
                    n.sem = sems[e]
                    n.val = c
        for q, lst in self.dma_nodes.items():
            if not lst:
                continue
            qs = dma_sems[q]
            cnt = [0] * len(qs)
            for i, n in enumerate(lst):
                s = i % len(qs)
                cnt[s] += 16
                n.sem = qs[s]
                n.val = cnt[s]
                n.signal = True
        out_dmas = self.out_dmas

        def run(e, engine):
            waited = {}
            for n in self.nodes[e]:
                need = {}
                for d in n.deps:
                    k = d.sem.num
                    if need.get(k, (0, None))[0] < d.val:
                        need[k] = (d.val, d.sem)
                for k, (v, s) in need.items():
                    if waited.get(k, 0) < v:
                        engine.wait_ge(s, v)
                        waited[k] = v
                ins = n.fn(engine)
                if n.signal:
                    ins.then_inc(n.sem, 16 if n.dma else 1)
            if e == "sp":
                for d in out_dmas:
                    if waited.get(d.sem.num, 0) < d.val:
                        engine.wait_ge(d.sem, d.val)
                        waited[d.sem.num] = d.val

        block.tensor(lambda eng: run("pe", eng))
        block.scalar(lambda eng: run("act", eng))
        block.vector(lambda eng: run("dve", eng))
        block.gpsimd(lambda eng: run("pool", eng))
        block.sync(lambda eng: run("sp", eng))


D = 1024
S = 2048
DFF = 2816
NKC = 8
NJ = 22
NS = 5
SLOT = 4096
EPS = 1e-6
NCONST = 224
C_GAIN = 0
C_CONVW = 56
C_DK = 80
C_B1 = 84
C_EPS = 88
C_TINY = 89
C_MUL = 96
C_ADD = 160
KSLOTS = (0, 1, 2, 4)
GELU_C = 1.5957691216057308
ALL_PHASES = ("ffn0", "conv", "ffn1", "kv", "ffn2", "nsa", "ffn3")


def _img_gu(w):
    a = w.reshape(8, 128, 2, 11, 2, 128)
    return np.ascontiguousarray(a.transpose(3, 1, 4, 2, 0, 5)).reshape(11, 128, 4096)


def _img_dn(w):
    a = w.reshape(2, 11, 128, 4, 2, 128)
    return np.ascontiguousarray(a.transpose(0, 3, 2, 4, 1, 5)).reshape(2, 4, 128, 2816)


def _img_cin(w):
    a = w.reshape(8, 128, 3, 8, 128)
    return np.ascontiguousarray(a.transpose(3, 1, 2, 0, 4)).reshape(8, 128, 3072)


def _img_sq(w):
    a = w.reshape(8, 128, 2, 4, 128)
    return np.ascontiguousarray(a.transpose(2, 1, 3, 0, 4)).reshape(2, 128, 4096)


def _fm(v):
    v = np.asarray(v, np.float32).reshape(-1, 8, 128)
    return np.ascontiguousarray(v.transpose(2, 0, 1)).reshape(128, -1)


class Builder:
    def __init__(self, phases, debug=False):
        self.phases = tuple(phases)
        self.debug = debug
        nc = self.nc = bass.Bass("TRN2", target_bir_lowering=False)
        self.P = Prog(nc)
        self.bank_rr = 0
        self.slot_rr = 0
        self.uid = 0
        dt = nc.dram_tensor
        self.d_x = dt("xT", [D, S], F32, kind="ExternalInput").ap()
        self.d_y = dt("yT", [D, S], F32, kind="ExternalOutput").ap()
        self.d_consts = dt("consts", [128, NCONST], F32, kind="ExternalInput").ap()
        self.d_wgu = dt("w_gu", [4, 11, 128, 4096], F32, kind="ExternalInput").ap()
        self.d_wdn = dt("w_dn", [4, 2, 4, 128, 2816], F32, kind="ExternalInput").ap()
        self.d_cin = dt("w_cin", [8, 128, 3072], F32, kind="ExternalInput").ap()
        self.d_cout = dt("w_cout", [2, 128, 4096], F32, kind="ExternalInput").ap()
        self.d_kvk = dt("w_kvk", [2, 128, 4096], F32, kind="ExternalInput").ap()
        self.d_kvv = dt("w_kvv", [128, 4096], F32, kind="ExternalInput").ap()
        self.d_w1 = dt("w_c1", [2, 2, 64, 4096], F32, kind="ExternalInput").ap()
        self.d_w2 = dt("w_c2", [128, 256], F32, kind="ExternalInput").ap()
        self.d_pos = dt("posT", [128, 32], F32, kind="ExternalInput").ap()
        self.d_eind = dt("m_eind", [32, S], F32, kind="ExternalInput").ap()
        self.d_valid = dt("m_valid", [128, S], F32, kind="ExternalInput").ap()
        self.d_tri = dt("m_tri", [128, 384], F32, kind="ExternalInput").ap()
        self.d_ovl = dt("m_ovl", [128, 32], F32, kind="ExternalInput").ap()
        self.d_wq = dt("w_q", [2, 128, 4096], F32, kind="ExternalInput").ap()
        self.d_wg = dt("w_g", [128, 384], F32, kind="ExternalInput").ap()
        self.d_wo = dt("w_o", [2, 128, 4096], F32, kind="ExternalInput").ap()
        A = nc.alloc_sbuf_tensor
        self.hT = A("hT", [128, NKC, S], F32)
        self.slots = [A(f"wslot{i}", [128, SLOT], BF16) for i in range(NS)]
        self.consts = A("consts_sb", [128, NCONST], F32)
        self.ones = A("ones_bf", [128, 128], BF16)
        self.ps = [nc.alloc_psum_tensor(f"psb{i}", [128, 512], F32) for i in range(8)]

    def key(self, name):
        self.uid += 1
        return (name, self.uid)

    def bank(self):
        b = self.bank_rr
        self.bank_rr = (b + 1) % 8
        return b

    def load_w(self, src_ap, nelem, parts=128, p0=0):
        i = self.slot_rr
        self.slot_rr = (i + 1) % NS
        sl = self.slots[i]
        self.P.op("pool", lambda e: e.dma_start(out=sl[p0:p0 + parts, 0:nelem], in_=src_ap),
                  writes=[("slot", i)], dma=True, arena=False)
        return sl, ("slot", i)

    def gcol(self, c):
        return self.consts[:, c:c + 1]

    def prologue(self):
        P = self.P
        P.op("sp", lambda e: e.dma_start(out=self.consts[:, :], in_=self.d_consts), writes=["consts"], dma=True, arena=False)
        P.op("dve", lambda e: e.memset(self.ones[:, :], 1.0), writes=["ones"], arena=False)
        xv = self.d_x.rearrange("(c p) t -> p c t", p=128)
        for c in range(NKC):
            P.op("sp", lambda e, c=c: e.dma_start(out=self.hT[:, c, :], in_=xv[:, c, :]),
                 writes=[("hT", c, q) for q in range(4)], dma=True, arena=False)

    def epilogue(self):
        P = self.P
        yv = self.d_y.rearrange("(c p) t -> p c t", p=128)
        for c in range(NKC):
            P.op("sp", lambda e, c=c: e.dma_start(out=yv[:, c, :], in_=self.hT[:, c, :]),
                 reads=[("hT", c, q) for q in range(4)], dma=True, arena=False, out=True)

    def rmsnorm_tile(self, q, gain_col, xn, xn_off, xn_key, work, bank=None):
        P = self.P
        sq, rstd = work
        tok = slice(q * 512, (q + 1) * 512)
        b = self.bank() if bank is None else bank
        for c in range(NKC):
            s = sq[c % 2]
            sk = ("sq", c % 2)
            P.op("act", lambda e, c=c, s=s: e.activation(out=s[:, :], in_=self.hT[:, c, tok], func=AF.Square),
                 reads=[("hT", c, q)], writes=[sk])
            P.op("pe", lambda e, c=c, s=s: e.matmul(self.ps[b][:, :], lhsT=self.ones[:, :], rhs=s[:, :],
                                                   start=(c == 0), stop=(c == NKC - 1)),
                 reads=[sk, "ones"], writes=[("ps", b)])
        P.op("act", lambda e: e.activation(out=rstd[:, :], in_=self.ps[b][:, :], func=AF.Sqrt,
                                           bias=self.gcol(C_EPS), scale=1.0 / D),
             reads=["consts"], writes=[("ps", b), "rstd"])
        P.op("dve", lambda e: e.reciprocal(out=rstd[:, :], in_=rstd[:, :]), reads=["rstd"], writes=["rstd"])
        for c in range(NKC):
            P.op("dve", lambda e, c=c: e.scalar_tensor_tensor(
                out=xn[:, c, xn_off:xn_off + 512], in0=self.hT[:, c, tok], scalar=self.gcol(gain_col + c),
                in1=rstd[:, :], op0=ALU.mult, op1=ALU.mult),
                reads=[("hT", c, q), "rstd", "consts"], writes=[(xn_key, c, xn_off // 512)])

    def ffn(self, li):
        nc, P = self.nc, self.P
        P.new_phase()
        with (
            nc.sbuf_tensor(f"ffn_xn{li}", [128, NKC, 1024], BF16) as xn,
            nc.sbuf_tensor(f"ffn_act{li}", [128, 11, 1024], BF16) as act,
            nc.sbuf_tensor(f"ffn_sq0{li}", [128, 512], BF16) as sq0,
            nc.sbuf_tensor(f"ffn_sq1{li}", [128, 512], BF16) as sq1,
            nc.sbuf_tensor(f"ffn_rstd{li}", [128, 512], F32) as rstd,
            nc.sbuf_tensor(f"ffn_sa0{li}", [128, 512], F32) as sa0,
            nc.sbuf_tensor(f"ffn_sa1{li}", [128, 512], F32) as sa1,
            nc.sbuf_tensor(f"ffn_sa2{li}", [128, 512], F32) as sa2,
        ):
            sas = [sa0, sa1, sa2]
            sa_rr = 0
            for half in range(2):
                for tt in range(2):
                    self.rmsnorm_tile(half * 2 + tt, C_GAIN + li * 8, xn, tt * 512, "ffn_xn", ((sq0, sq1), rstd))
                for jh in range(2):
                    jlist = list(range(jh * 11, jh * 11 + 11))
                    loaded = {}
                    for j in jlist:
                        jp, jj = divmod(j, 2)
                        if jp not in loaded:
                            loaded[jp] = self.load_w(self.d_wgu[li, jp], 4096)
                        sl, sk = loaded[jp]
                        jl = j - jh * 11
                        for tt in range(2):
                            ba, bb = self.bank(), self.bank()
                            tk = slice(tt * 512, (tt + 1) * 512)
                            for ab, b in ((0, ba), (1, bb)):
                                for kc in range(NKC):
                                    o = ((jj * 2 + ab) * 8 + kc) * 128
                                    P.op("pe", lambda e, b=b, o=o, kc=kc, sl=sl, tk=tk: e.matmul(
                                        self.ps[b][:, :], lhsT=sl[:, o:o + 128], rhs=xn[:, kc, tk],
                                        start=(kc == 0), stop=(kc == NKC - 1)),
                                        reads=[sk, ("ffn_xn", kc, tt)], writes=[("ps", b)])
                            sa = sas[sa_rr]
                            sak = ("ffn_sa", sa_rr)
                            sa_rr = (sa_rr + 1) % 3
                            P.op("act", lambda e, sa=sa, ba=ba: e.activation(out=sa[:, :], in_=self.ps[ba][:, :], func=AF.Silu),
                                 writes=[("ps", ba), sak])
                            P.op("dve", lambda e, sa=sa, bb=bb, jl=jl, tk=tk: e.tensor_tensor(
                                out=act[:, jl, tk], in0=sa[:, :], in1=self.ps[bb][:, :], op=ALU.mult),
                                reads=[sak], writes=[("ps", bb), ("ffn_act", jl, tt)])
                    for mp in range(4):
                        sl, sk = self.load_w(self.d_wdn[li, jh, mp], 2816)
                        for ml in range(2):
                            m = mp * 2 + ml
                            for tt in range(2):
                                b = self.bank()
                                q = half * 2 + tt
                                tk = slice(tt * 512, (tt + 1) * 512)
                                for jl in range(11):
                                    o = (ml * 11 + jl) * 128
                                    P.op("pe", lambda e, b=b, o=o, jl=jl, sl=sl, tk=tk: e.matmul(
                                        self.ps[b][:, :], lhsT=sl[:, o:o + 128], rhs=act[:, jl, tk],
                                        start=(jl == 0), stop=(jl == 10)),
                                        reads=[sk, ("ffn_act", jl, tt)], writes=[("ps", b)])
                                P.op("dve", lambda e, b=b, m=m, q=q: e.scalar_tensor_tensor(
                                    out=self.hT[:, m, q * 512:(q + 1) * 512], in0=self.ps[b][:, :], scalar=0.5,
                                    in1=self.hT[:, m, q * 512:(q + 1) * 512], op0=ALU.mult, op1=ALU.add),
                                    reads=[], writes=[("ps", b), ("hT", m, q)])

    def conv(self):
        nc, P = self.nc, self.P
        P.new_phase()
        with (
            nc.sbuf_tensor("cv_xn", [128, NKC, S], BF16) as xn,
            nc.sbuf_tensor("cv_gated", [128, NKC, S], BF16) as gated,
            nc.sbuf_tensor("cv_sq0", [128, 512], BF16) as sq0,
            nc.sbuf_tensor("cv_sq1", [128, 512], BF16) as sq1,
            nc.sbuf_tensor("cv_rstd", [128, 512], F32) as rstd,
            nc.sbuf_tensor("cv_csb", [128, 512], F32) as csb,
            nc.sbuf_tensor("cv_v", [128, 2, 514], F32) as vbuf,
            nc.sbuf_tensor("cv_acc", [128, 2, 512], F32) as accb,
        ):
            for q in range(4):
                self.rmsnorm_tile(q, C_GAIN + 32, xn, q * 512, "cv_xn", ((sq0, sq1), rstd))
            vi = 0
            for m in range(NKC):
                sl, sk = self.load_w(self.d_cin[m], 3072)
                for q in range(4):
                    tk = slice(q * 512, (q + 1) * 512)
                    bs = [self.bank(), self.bank(), self.bank()]
                    for s3 in range(3):
                        for kc in range(NKC):
                            o = (s3 * 8 + kc) * 128
                            P.op("pe", lambda e, b=bs[s3], o=o, kc=kc, sl=sl, tk=tk: e.matmul(
                                self.ps[b][:, :], lhsT=sl[:, o:o + 128], rhs=xn[:, kc, tk],
                                start=(kc == 0), stop=(kc == NKC - 1)),
                                reads=[sk, ("cv_xn", kc, q)], writes=[("ps", bs[s3])])
                    v = vbuf[:, vi, :]
                    vk = ("cv_v", vi)
                    vprev = vbuf[:, 1 - vi, :]
                    vpk = ("cv_v", 1 - vi)
                    acc = accb[:, vi, :]
                    ak = ("cv_acc", vi)
                    vi = 1 - vi
                    if q == 0:
                        P.op("dve", lambda e, v=v: e.memset(v[:, 0:2], 0.0), writes=[vk])
                    else:
                        P.op("dve", lambda e, v=v, vprev=vprev: e.tensor_copy(out=v[:, 0:2], in_=vprev[:, 512:514]),
                             reads=[vpk], writes=[vk])
                    P.op("act", lambda e, b=bs[1]: e.activation(out=csb[:, :], in_=self.ps[b][:, :], func=AF.Copy),
                         writes=[("ps", bs[1]), "cv_csb"])
                    P.op("dve", lambda e, v=v, b=bs[2]: e.tensor_tensor(out=v[:, 2:514], in0=csb[:, :], in1=self.ps[b][:, :], op=ALU.mult),
                         reads=["cv_csb"], writes=[("ps", bs[2]), vk])
                    P.op("dve", lambda e, v=v, acc=acc, m=m: e.tensor_scalar(
                        out=acc, in0=v[:, 0:512], scalar1=self.gcol(C_CONVW + 0 * 8 + m), scalar2=None, op0=ALU.mult),
                        reads=[vk, "consts"], writes=[ak])
                    for k in (1, 2):
                        P.op("dve", lambda e, v=v, acc=acc, m=m, k=k: e.scalar_tensor_tensor(
                            out=acc, in0=v[:, k:k + 512], scalar=self.gcol(C_CONVW + k * 8 + m), in1=acc,
                            op0=ALU.mult, op1=ALU.add),
                            reads=[vk, ak, "consts"], writes=[ak])
                    P.op("dve", lambda e, acc=acc, b=bs[0], m=m, tk=tk: e.tensor_tensor(
                        out=gated[:, m, tk], in0=acc, in1=self.ps[b][:, :], op=ALU.mult),
                        reads=[ak], writes=[("ps", bs[0]), ("cv_gated", m, q)])
            for mh in range(2):
                sl, sk = self.load_w(self.d_cout[mh], 4096)
                for ml in range(4):
                    mo = mh * 4 + ml
                    for q in range(4):
                        tk = slice(q * 512, (q + 1) * 512)
                        b = self.bank()
                        for kc in range(NKC):
                            o = (ml * 8 + kc) * 128
                            P.op("pe", lambda e, b=b, o=o, kc=kc, sl=sl, tk=tk: e.matmul(
                                self.ps[b][:, :], lhsT=sl[:, o:o + 128], rhs=gated[:, kc, tk],
                                start=(kc == 0), stop=(kc == NKC - 1)),
                                reads=[sk, ("cv_gated", kc, q)], writes=[("ps", b)])
                        P.op("dve", lambda e, b=b, mo=mo, tk=tk: e.tensor_tensor(
                            out=self.hT[:, mo, tk], in0=self.ps[b][:, :], in1=self.hT[:, mo, tk], op=ALU.add),
                            writes=[("ps", b), ("hT", mo, q)])

    def dknorm(self, b, n, gain_col, dest, dest_keys, work, b2):
        P = self.P
        sqd, rs = work
        src = self.ps[b][0:64, 0:n]
        P.op("act", lambda e: e.activation(out=sqd[0:64, 0:n], in_=src, func=AF.Square),
             writes=[("ps", b), "dk_sq"])
        P.op("pe", lambda e: e.matmul(self.ps[b2][0:64, 0:n], lhsT=self.ones[0:64, 0:64], rhs=sqd[0:64, 0:n],
                                      start=True, stop=True),
             reads=["dk_sq", "ones"], writes=[("ps", b2)])
        P.op("act", lambda e: e.activation(out=rs[0:64, 0:n], in_=self.ps[b2][0:64, 0:n], func=AF.Sqrt,
                                           bias=self.consts[0:64, C_EPS:C_EPS + 1], scale=1.0 / 64),
             reads=["consts"], writes=[("ps", b2), "dk_rs"])
        P.op("dve", lambda e: e.reciprocal(out=rs[0:64, 0:n], in_=rs[0:64, 0:n]), reads=["dk_rs"], writes=["dk_rs"])
        rsv = rs[0:64, 0:n]
        srcv = src
        if len(dest.shape) == 3:
            g_ = dest.shape[1]
            rsv = rsv.rearrange("p (g n) -> p g n", g=g_)
            srcv = srcv.rearrange("p (g n) -> p g n", g=g_)
        P.op("dve", lambda e: e.scalar_tensor_tensor(out=dest, in0=srcv, scalar=self.consts[0:64, gain_col:gain_col + 1],
                                                     in1=rsv, op0=ALU.mult, op1=ALU.mult),
             reads=["dk_rs", "consts"], writes=[("ps", b)] + list(dest_keys))

    def kv(self):
        nc, P = self.nc, self.P
        P.new_phase()
        A = nc.alloc_sbuf_tensor
        self.kselT = A("kselT", [96, 4, S], BF16)
        self.kwinT = A("kwinT", [64, 4, S], BF16)
        self.vsel = A("vsel", [128, 16, 4, 65], BF16)
        self.vwin = A("vwin", [128, 16, 4, 65], BF16)
        self.kcmpT = A("kcmpT", [64, 4, 128], BF16)
        self.vcmp = A("vcmp", [128, 4, 97], BF16)
        kselT, kwinT, vsel, vwin, kcmpT, vcmp = self.kselT, self.kwinT, self.vsel, self.vwin, self.kcmpT, self.vcmp
        with (
            nc.sbuf_tensor("kv_xn", [128, NKC, 512], BF16) as xn,
            nc.sbuf_tensor("kv_raw", [128, 4, S], BF16) as rawT,
            nc.sbuf_tensor("kv_sq0", [128, 512], BF16) as sq0,
            nc.sbuf_tensor("kv_sq1", [128, 512], BF16) as sq1,
            nc.sbuf_tensor("kv_rstd", [128, 512], F32) as rstd,
            nc.sbuf_tensor("kv_sqd", [64, 512], BF16) as sqd,
            nc.sbuf_tensor("kv_rs", [64, 512], F32) as rs,
            nc.sbuf_tensor("kv_pos", [128, 32], BF16) as posT,
            nc.sbuf_tensor("kv_w2", [128, 256], BF16) as w2t,
            nc.sbuf_tensor("kv_u", [128, 508], F32) as ubuf,
            nc.sbuf_tensor("kv_t", [128, 508], F32) as tbuf,
            nc.sbuf_tensor("kv_cb", [128, 1], F32) as cb,
            nc.sbuf_tensor("kv_h1", [128, 2, 508], BF16) as h1,
        ):
            for g in range(4):
                P.op("pool", lambda e, g=g: e.dma_start(out=kselT[64:96, g, :], in_=self.d_eind), writes=[("kselE", g)], dma=True)
                P.op("pool", lambda e, g=g: e.dma_start(out=vcmp[:, g, 65:97], in_=self.d_ovl), writes=[("vcmp_o", g)], dma=True)
            P.op("pool", lambda e: e.dma_start(out=posT[:, :], in_=self.d_pos), writes=["kv_pos"], dma=True)
            P.op("pool", lambda e: e.dma_start(out=w2t[:, :], in_=self.d_w2), writes=["kv_w2"], dma=True)
            P.op("dve", lambda e: e.memset(kcmpT[:, :, :], 0.0), writes=["kcmpT"])
            P.op("dve", lambda e: e.memset(vcmp[:, :, 0:64], 0.0), writes=[("vcmp_v", g) for g in range(4)])
            P.op("dve", lambda e: e.memset(vcmp[:, :, 64:65], 1.0), writes=["vcmp_1"])
            P.op("dve", lambda e: e.memset(vsel[:, :, :, 64:65], 1.0), writes=["vsel_1"])
            P.op("dve", lambda e: e.memset(vwin[:, :, :, 64:65], 1.0), writes=["vwin_1"])
            slk = [self.load_w(self.d_kvk[i], 4096) for i in range(2)]
            slv, slvk = self.load_w(self.d_kvv, 4096)
            for q in range(4):
                tk = slice(q * 512, (q + 1) * 512)
                self.rmsnorm_tile(q, C_GAIN + 48, xn, 0, "kv_xn", ((sq0, sq1), rstd))
                slA, skA = slk[0]
                for g in range(4):
                    b = self.bank()
                    for kc in range(NKC):
                        o = (g * 8 + kc) * 128
                        P.op("pe", lambda e, b=b, o=o, kc=kc: e.matmul(
                            self.ps[b][:, :], lhsT=slA[:, o:o + 128], rhs=xn[:, kc, :],
                            start=(kc == 0), stop=(kc == NKC - 1)),
                            reads=[skA, ("kv_xn", kc, 0)], writes=[("ps", b)])
                    P.op("act", lambda e, b=b, g=g, tk=tk: e.activation(
                        out=rawT[:, g, tk], in_=self.ps[b][:, :], func=AF.Copy),
                        writes=[("ps", b), ("kv_raw", g, q)])
                slB, skB = slk[1]
                for blk in range(8):
                    si, g = divmod(blk, 4)
                    b = self.bank()
                    for kc in range(NKC):
                        o = (blk * 8 + kc) * 64
                        P.op("pe", lambda e, b=b, o=o, kc=kc: e.matmul(
                            self.ps[b][0:64, :], lhsT=slB[:, o:o + 64], rhs=xn[:, kc, :],
                            start=(kc == 0), stop=(kc == NKC - 1)),
                            reads=[skB, ("kv_xn", kc, 0)], writes=[("ps", b)])
                    if si == 0:
                        self.dknorm(b, 512, C_DK + 1, kselT[0:64, g, tk], [("kselT", g, q)], (sqd, rs), self.bank())
                    else:
                        self.dknorm(b, 512, C_DK + 2, kwinT[0:64, g, tk], [("kwinT", g, q)], (sqd, rs), self.bank())
                for sub in range(4):
                    tt = q * 4 + sub
                    b = self.bank()
                    for kc in range(NKC):
                        P.op("pe", lambda e, b=b, kc=kc, sub=sub: e.matmul(
                            self.ps[b][:, :], lhsT=xn[:, kc, sub * 128:(sub + 1) * 128], rhs=slv[:, kc * 512:(kc + 1) * 512],
                            start=(kc == 0), stop=(kc == NKC - 1)),
                            reads=[slvk, ("kv_xn", kc, 0)], writes=[("ps", b)])
                    P.op("act", lambda e, b=b, tt=tt: e.activation(
                        out=vsel[:, tt, :, 0:64], in_=self.ps[b][:, 0:256].rearrange("p (g d) -> p g d", g=4), func=AF.Copy),
                        writes=[("ps", b), ("vsel", tt)])
                    P.op("dve", lambda e, b=b, tt=tt: e.tensor_copy(
                        out=vwin[:, tt, :, 0:64], in_=self.ps[b][:, 256:512].rearrange("p (g d) -> p g d", g=4)),
                        writes=[("ps", b), ("vwin", tt)])
            raw_keys = lambda s_: [("kv_raw", g, q) for g in range(4) for q in range(4)]
            for s_ in range(2):
                pl = slice(s_ * 64, s_ * 64 + 64)
                w1s = [self.load_w(self.d_w1[s_, hf], 4096, parts=64, p0=s_ * 64) for hf in range(2)]
                for hc in range(2):
                    b = self.bank()
                    b1 = self.bank()
                    for l in range(32):
                        sl, sk = w1s[l // 16]
                        o = (l % 16) * 256 + hc * 128
                        P.op("pe", lambda e, b=b, sl=sl, o=o, l=l, pl=pl: e.matmul(
                            self.ps[b][:, 0:508], lhsT=sl[pl, o:o + 128],
                            rhs=rawT[pl, :, l:l + 16 * 126 + 1:16], start=(l == 0), stop=(l == 31)),
                            reads=[sk] + raw_keys(s_), writes=[("ps", b)])
                    for l in range(32):
                        sl, sk = w1s[l // 16]
                        o = (l % 16) * 256 + hc * 128
                        P.op("pe", lambda e, b1=b1, sl=sl, o=o, l=l, pl=pl: e.matmul(
                            self.ps[b1][:, 0:1], lhsT=sl[pl, o:o + 128],
                            rhs=posT[pl, l:l + 1], start=(l == 0), stop=(l == 31)),
                            reads=[sk, "kv_pos"], writes=[("ps", b1)])
                    P.op("dve", lambda e, b1=b1, s_=s_, hc=hc: e.tensor_scalar(
                        out=cb[:, :], in0=self.ps[b1][:, 0:1], scalar1=self.gcol(C_B1 + s_ * 2 + hc), scalar2=None, op0=ALU.add),
                        reads=["consts"], writes=[("ps", b1), "kv_cb"])
                    P.op("dve", lambda e, b=b: e.tensor_scalar(
                        out=ubuf[:, :], in0=self.ps[b][:, 0:508], scalar1=cb[:, 0:1], scalar2=None, op0=ALU.add),
                        reads=["kv_cb"], writes=[("ps", b), "kv_u"])
                    P.op("dve", lambda e: e.tensor_tensor(out=tbuf[:, :], in0=ubuf[:, :], in1=ubuf[:, :], op=ALU.mult),
                         reads=["kv_u"], writes=["kv_t"])
                    P.op("dve", lambda e: e.tensor_scalar(out=tbuf[:, :], in0=tbuf[:, :], scalar1=0.044715, scalar2=1.0,
                                                          op0=ALU.mult, op1=ALU.add), reads=["kv_t"], writes=["kv_t"])
                    P.op("dve", lambda e: e.tensor_tensor(out=tbuf[:, :], in0=tbuf[:, :], in1=ubuf[:, :], op=ALU.mult),
                         reads=["kv_t", "kv_u"], writes=["kv_t"])
                    P.op("act", lambda e: e.activation(out=tbuf[:, :], in_=tbuf[:, :], func=AF.Sigmoid, scale=GELU_C),
                         reads=["kv_t"], writes=["kv_t"])
                    P.op("dve", lambda e, hc=hc: e.tensor_tensor(out=h1[:, hc, :], in0=tbuf[:, :], in1=ubuf[:, :], op=ALU.mult),
                         reads=["kv_t", "kv_u"], writes=[("kv_h1", hc)])
                if s_ == 0:
                    b = self.bank()
                    for hc in range(2):
                        P.op("pe", lambda e, b=b, hc=hc: e.matmul(
                            self.ps[b][0:64, 0:508], lhsT=w2t[:, hc * 64:(hc + 1) * 64], rhs=h1[:, hc, :],
                            start=(hc == 0), stop=(hc == 1)),
                            reads=["kv_w2", ("kv_h1", hc)], writes=[("ps", b)])
                    self.dknorm(b, 508, C_DK + 0, kcmpT[0:64, :, 0:127], ["kcmpT"], (sqd, rs), self.bank())
                else:
                    for g in range(4):
                        b = self.bank()
                        for hc in range(2):
                            P.op("pe", lambda e, b=b, hc=hc, g=g: e.matmul(
                                self.ps[b][0:127, 0:64], lhsT=h1[:, hc, g * 127:(g + 1) * 127],
                                rhs=w2t[:, (2 + hc) * 64:(3 + hc) * 64], start=(hc == 0), stop=(hc == 1)),
                                reads=["kv_w2", ("kv_h1", hc)], writes=[("ps", b)])
                        P.op("act", lambda e, b=b, g=g: e.activation(out=vcmp[0:127, g, 0:64], in_=self.ps[b][0:127, 0:64], func=AF.Copy),
                             writes=[("ps", b), ("vcmp_v", g)])
            if self.debug:
                for nm, t in (("kselT", kselT), ("kwinT", kwinT), ("vsel", vsel), ("vwin", vwin), ("kcmpT", kcmpT), ("vcmp", vcmp)):
                    shp = list(t.shape)
                    dd = nc.dram_tensor("dbg_" + nm, shp, BF16, kind="ExternalOutput").ap()
                    idx = tuple(slice(None) for _ in shp)
                    P.op("sp", lambda e, dd=dd, t=t, idx=idx: e.dma_start(out=dd[idx], in_=t[idx]),
                         reads=[k for k in list(P.last_w.keys()) if isinstance(k, (tuple, str))], dma=True, out=True)

    def nsa(self):
        nc, P = self.nc, self.P
        P.new_phase()
        kselT, kwinT, vsel, vwin, kcmpT, vcmp = self.kselT, self.kwinT, self.vsel, self.vwin, self.kcmpT, self.vcmp
        ps = self.ps
        B_S = (0, 1, 2)
        B_CMP, B_SEL, B_WIN, B_A, B_B = 3, 4, 5, 6, 7
        with ExitStack() as st:
            xn = st.enter_context(nc.sbuf_tensor("ns_xn", [128, NKC, 512], BF16))
            sq0 = st.enter_context(nc.sbuf_tensor("ns_sq0", [128, 512], BF16))
            sq1 = st.enter_context(nc.sbuf_tensor("ns_sq1", [128, 512], BF16))
            rstd = st.enter_context(nc.sbuf_tensor("ns_rstd", [128, 512], F32))
            sqd = st.enter_context(nc.sbuf_tensor("ns_sqd", [64, 512], BF16))
            rs = st.enter_context(nc.sbuf_tensor("ns_rs", [64, 512], F32))
            validT = st.enter_context(nc.sbuf_tensor("ns_valid", [128, S], BF16))
            tri = st.enter_context(nc.sbuf_tensor("ns_tri", [128, 384], BF16))
            gate = st.enter_context(nc.sbuf_tensor("ns_gate", [128, 4, 48], F32))
            qaug = st.enter_context(nc.sbuf_tensor("ns_qaug", [96, 4, 512], BF16))
            ptb = st.enter_context(nc.sbuf_tensor("ns_pt", [128, 4, 512], BF16))
            oacc = st.enter_context(nc.sbuf_tensor("ns_oacc", [128, 4, 4, 64], F32))
            tmpo = st.enter_context(nc.sbuf_tensor("ns_tmp", [128, 4, 64], F32))
            rd = st.enter_context(nc.sbuf_tensor("ns_rd", [128, 3, 4], F32))
            coef = st.enter_context(nc.sbuf_tensor("ns_coef", [128, 3, 4], F32))
            imp = st.enter_context(nc.sbuf_tensor("ns_imp", [128, 4, 32], F32))
            impt = st.enter_context(nc.sbuf_tensor("ns_impt", [128, 4, 32], F32))
            m8 = st.enter_context(nc.sbuf_tensor("ns_m8", [128, 4, 16], F32))
            biasb = st.enter_context(nc.sbuf_tensor("ns_bias", [128, 4, 96], BF16))
            opair = st.enter_context(nc.sbuf_tensor("ns_opair", [128, 2, 4, 128], BF16))
            oT = st.enter_context(nc.sbuf_tensor("ns_oT", [128, NKC, 512], BF16))
            P.op("pool", lambda e: e.dma_start(out=validT[:, :], in_=self.d_valid), writes=["ns_valid"], dma=True)
            P.op("pool", lambda e: e.dma_start(out=tri[:, :], in_=self.d_tri), writes=["ns_tri", "ns_ident"], dma=True)
            ident = tri[:, 256:384]
            P.op("dve", lambda e: e.memset(qaug[:, :, :], 0.0), writes=[("qaug", i) for i in range(4)] + [("qbias", i) for i in range(4)])
            P.op("dve", lambda e: e.memset(biasb[:, :, :], 0.0), writes=["ns_bias"])
            pt_rr = 0
            s_rr = 0
            for Q in range(4):
                tq = slice(Q * 512, (Q + 1) * 512)
                self.rmsnorm_tile(Q, C_GAIN + 40, xn, 0, "ns_xn", ((sq0, sq1), rstd), bank=B_A)
                xn_keys = [("ns_xn", kc, 0) for kc in range(NKC)]
                slg, skg = self.load_w(self.d_wg, 384)
                for sub in range(4):
                    for kc in range(NKC):
                        P.op("pe", lambda e, kc=kc, sub=sub: e.matmul(
                            ps[B_B][:, sub * 48:(sub + 1) * 48], lhsT=xn[:, kc, sub * 128:(sub + 1) * 128],
                            rhs=slg[:, kc * 48:(kc + 1) * 48], start=(kc == 0), stop=(kc == NKC - 1)),
                            reads=[skg, ("ns_xn", kc, 0)], writes=[("ps", B_B)])
                P.op("act", lambda e: e.activation(out=gate[:, :, :], in_=ps[B_B][:, 0:192].rearrange("p (s c) -> p s c", s=4),
                                                   func=AF.Sigmoid), writes=[("ps", B_B), "ns_gate"])
                slq = [self.load_w(self.d_wq[i], 4096) for i in range(2)]
                for g in range(4):
                    for hl in range(4):
                        h = g * 4 + hl
                        slh, skh = slq[h // 8]
                        for kc in range(NKC):
                            o = ((h % 8) * 8 + kc) * 64
                            P.op("pe", lambda e, o=o, kc=kc, slh=slh: e.matmul(
                                ps[B_A][0:64, :], lhsT=slh[:, o:o + 64], rhs=xn[:, kc, :],
                                start=(kc == 0), stop=(kc == NKC - 1)),
                                reads=[skh, ("ns_xn", kc, 0)], writes=[("ps", B_A)])
                        self.dknorm(B_A, 512, C_DK + 3, qaug[0:64, hl, :], [("qaug", hl)], (sqd, rs), B_B)
                        bS = B_S[s_rr]; s_rr = (s_rr + 1) % 3
                        pt = ptb[:, pt_rr, :]; ptk = ("ns_pt", pt_rr); pt_rr = (pt_rr + 1) % 4
                        P.op("pe", lambda e, bS=bS, g=g, hl=hl: e.matmul(
                            ps[bS][0:127, :], lhsT=kcmpT[0:64, g, 0:127], rhs=qaug[0:64, hl, :], start=True, stop=True),
                            reads=["kcmpT", ("qaug", hl)], writes=[("ps", bS)])
                        P.op("act", lambda e, bS=bS, pt=pt: e.activation(out=pt[0:127, :], in_=ps[bS][0:127, :], func=AF.Exp, scale=0.125),
                             writes=[("ps", bS), ptk])
                        P.op("pool", lambda e, pt=pt, tq=tq: e.tensor_tensor(out=pt[0:127, :], in0=pt[0:127, :], in1=validT[0:127, tq], op=ALU.mult),
                             reads=["ns_valid"], writes=[ptk])
                        for sub in range(4):
                            P.op("pe", lambda e, sub=sub, pt=pt, g=g: e.matmul(
                                ps[B_CMP][:, sub * 97:(sub + 1) * 97], lhsT=pt[0:127, sub * 128:(sub + 1) * 128],
                                rhs=vcmp[0:127, g, :], start=True, stop=True),
                                reads=[ptk, ("vcmp_v", g), ("vcmp_o", g), "vcmp_1"], writes=[("ps", B_CMP)])
                        pc = ps[B_CMP][:, 0:388].rearrange("p (s c) -> p s c", s=4)
                        self.branch_coef(pc[:, :, 64:65], rd, coef, gate, h, 0, B_CMP)
                        P.op("dve", lambda e, pc=pc, hl=hl: e.tensor_tensor(
                            out=oacc[:, hl, :, :], in0=pc[:, :, 0:64], in1=coef[:, 0, :].unsqueeze(2).to_broadcast([128, 4, 64]), op=ALU.mult),
                            reads=["ns_coef"], writes=[("ps", B_CMP), ("ns_oacc", hl)])
                        if Q >= 2:
                            dst = imp if hl == 0 else impt
                            dk_ = "ns_imp" if hl == 0 else "ns_impt"
                            P.op("dve", lambda e, pc=pc, dst=dst: e.tensor_tensor(
                                out=dst[:, :, :], in0=pc[:, :, 65:97], in1=rd[:, 0, :].unsqueeze(2).to_broadcast([128, 4, 32]), op=ALU.mult),
                                reads=["ns_rd"], writes=[("ps", B_CMP), dk_])
                            if hl > 0:
                                P.op("dve", lambda e: e.tensor_tensor(out=imp[:, :, :], in0=imp[:, :, :], in1=impt[:, :, :], op=ALU.add),
                                     reads=["ns_impt"], writes=["ns_imp"])
                    if Q >= 2:
                        for sub in range(4):
                            tile = Q * 4 + sub
                            c0 = 31 - 2 * tile
                            P.op("dve", lambda e, sub=sub, c0=c0: e.tensor_tensor(
                                out=imp[:, sub, :], in0=imp[:, sub, :], in1=self.consts[:, C_MUL + c0:C_MUL + c0 + 32], op=ALU.mult),
                                reads=["consts"], writes=["ns_imp"])
                            P.op("dve", lambda e, sub=sub, c0=c0: e.tensor_tensor(
                                out=imp[:, sub, :], in0=imp[:, sub, :], in1=self.consts[:, C_ADD + c0:C_ADD + c0 + 32], op=ALU.add),
                                reads=["consts"], writes=["ns_imp"])
                        P.op("dve", lambda e: e.memset(imp[:, :, 0:1], 3e9), writes=["ns_imp"])
                        for sub in range(4):
                            P.op("dve", lambda e, sub=sub: e.max(out=m8[:, sub, 0:8], in_=imp[:, sub, :]), reads=["ns_imp"], writes=["ns_m8"])
                            P.op("dve", lambda e, sub=sub: e.match_replace(out=impt[:, sub, :], in_to_replace=m8[:, sub, 0:8],
                                                                           in_values=imp[:, sub, :], imm_value=-1e30),
                                 reads=["ns_imp", "ns_m8"], writes=["ns_impt"])
                            P.op("dve", lambda e, sub=sub: e.max(out=m8[:, sub, 8:16], in_=impt[:, sub, :]), reads=["ns_impt"], writes=["ns_m8"])
                            P.op("dve", lambda e, sub=sub: e.tensor_scalar(
                                out=biasb[:, sub, 64:96], in0=imp[:, sub, :], scalar1=m8[:, sub, 15:16], scalar2=-16384.0,
                                op0=ALU.is_lt, op1=ALU.mult), reads=["ns_imp", "ns_m8"], writes=["ns_bias"])
                        pb = ps[B_A][:, :].bitcast(BF16)
                        for sub in range(4):
                            P.op("pe", lambda e, sub=sub, pb=pb: e.transpose(out=pb[0:96, sub * 128:(sub + 1) * 128], in_=biasb[:, sub, :], identity=ident),
                                 reads=["ns_bias", "ns_ident"], writes=[("ps", B_A)])
                        for hl in range(4):
                            eng = "act" if hl % 2 == 0 else "dve"
                            if eng == "act":
                                P.op("act", lambda e, hl=hl, pb=pb: e.activation(out=qaug[64:96, hl, :], in_=pb[64:96, 0:512], func=AF.Copy),
                                     writes=[("ps", B_A), ("qbias", hl)])
                            else:
                                P.op("dve", lambda e, hl=hl, pb=pb: e.tensor_copy(out=qaug[64:96, hl, :], in_=pb[64:96, 0:512]),
                                     writes=[("ps", B_A), ("qbias", hl)])
                    for hl in range(4):
                        h = g * 4 + hl
                        for br, (bank_o, kT, kK, vv, vkey) in enumerate(((B_SEL, kselT, 96, vsel, "vsel"), (B_WIN, kwinT, 64, vwin, "vwin"))):
                            P.op("dve", lambda e, bank_o=bank_o: e.memset(ps[bank_o][:, 0:260], 0.0), writes=[("ps", bank_o)])
                            kc_lo = 0 if br == 0 else max(0, 4 * Q - 4)
                            for kc in range(kc_lo, 4 * Q + 4):
                                d = kc - 4 * Q
                                lo = max(0, d)
                                hi = 3 if br == 0 else min(3, d + 4)
                                f0, f1 = lo * 128, (hi + 1) * 128
                                bS = B_S[s_rr]; s_rr = (s_rr + 1) % 3
                                pt = ptb[:, pt_rr, :]; ptk = ("ns_pt", pt_rr); pt_rr = (pt_rr + 1) % 4
                                kk = [("kselT", g, kc // 4), ("kselE", g)] if br == 0 else [("kwinT", g, kc // 4)]
                                qk = [("qaug", hl), ("qbias", hl)] if br == 0 else [("qaug", hl)]
                                P.op("pe", lambda e, bS=bS, kT=kT, kK=kK, g=g, kc=kc, hl=hl, f0=f0, f1=f1: e.matmul(
                                    ps[bS][:, f0:f1], lhsT=kT[0:kK, g, kc * 128:(kc + 1) * 128], rhs=qaug[0:kK, hl, f0:f1],
                                    start=True, stop=True), reads=kk + qk, writes=[("ps", bS)])
                                P.op("act", lambda e, bS=bS, pt=pt, f0=f0, f1=f1: e.activation(
                                    out=pt[:, f0:f1], in_=ps[bS][:, f0:f1], func=AF.Exp, scale=0.125),
                                    writes=[("ps", bS), ptk])
                                if d >= 0:
                                    P.op("pool", lambda e, pt=pt, d=d: e.tensor_tensor(
                                        out=pt[:, d * 128:(d + 1) * 128], in0=pt[:, d * 128:(d + 1) * 128], in1=tri[:, 0:128], op=ALU.mult),
                                        reads=["ns_tri"], writes=[ptk])
                                if br == 1 and d + 4 <= 3:
                                    P.op("pool", lambda e, pt=pt, d=d: e.tensor_tensor(
                                        out=pt[:, (d + 4) * 128:(d + 5) * 128], in0=pt[:, (d + 4) * 128:(d + 5) * 128], in1=tri[:, 128:256], op=ALU.mult),
                                        reads=["ns_tri"], writes=[ptk])
                                for sub in range(lo, hi + 1):
                                    P.op("pe", lambda e, bank_o=bank_o, sub=sub, pt=pt, vv=vv, kc=kc, g=g, Q=Q: e.matmul(
                                        ps[bank_o][:, sub * 65:(sub + 1) * 65], lhsT=pt[:, sub * 128:(sub + 1) * 128],
                                        rhs=vv[:, kc, g, :], start=False, stop=(kc == 4 * Q + sub), skip_group_check=True),
                                        reads=[ptk, (vkey, kc), vkey + "_1"], writes=[("ps", bank_o)])
                            po = ps[bank_o][:, 0:260].rearrange("p (s c) -> p s c", s=4)
                            self.branch_coef(po[:, :, 64:65], rd, coef, gate, h, br + 1, bank_o)
                            P.op("dve", lambda e, po=po, br=br: e.tensor_tensor(
                                out=tmpo[:, :, :], in0=po[:, :, 0:64], in1=coef[:, br + 1, :].unsqueeze(2).to_broadcast([128, 4, 64]), op=ALU.mult),
                                reads=["ns_coef"], writes=[("ps", bank_o), "ns_tmpo"])
                            if br == 0:
                                P.op("dve", lambda e, hl=hl: e.tensor_tensor(out=oacc[:, hl, :, :], in0=oacc[:, hl, :, :], in1=tmpo[:, :, :], op=ALU.add),
                                     reads=["ns_tmpo"], writes=[("ns_oacc", hl)])
                            else:
                                pi = (h // 2) % 2
                                P.op("dve", lambda e, hl=hl, pi=pi, h=h: e.tensor_tensor(
                                    out=opair[:, pi, :, (h % 2) * 64:(h % 2) * 64 + 64], in0=oacc[:, hl, :, :], in1=tmpo[:, :, :], op=ALU.add),
                                    reads=["ns_tmpo", ("ns_oacc", hl)], writes=[("ns_opair", pi, h % 2)])
                        if h % 2 == 1:
                            c = h // 2
                            pi = c % 2
                            pb = ps[B_B][:, :].bitcast(BF16)
                            for sub in range(4):
                                P.op("pe", lambda e, sub=sub, pi=pi, pb=pb: e.transpose(
                                    out=pb[:, sub * 128:(sub + 1) * 128], in_=opair[:, pi, sub, :], identity=ident),
                                    reads=[("ns_opair", pi, 0), ("ns_opair", pi, 1), "ns_ident"], writes=[("ps", B_B)])
                            P.op("act", lambda e, c=c, pb=pb: e.activation(out=oT[:, c, :], in_=pb[:, 0:512], func=AF.Copy),
                                 writes=[("ps", B_B), ("ns_oT", c)])
                slo = [self.load_w(self.d_wo[i], 4096) for i in range(2)]
                for mo in range(NKC):
                    sl, sk = slo[mo // 4]
                    b = B_A if mo % 2 == 0 else B_B
                    for kc in range(NKC):
                        o = ((mo % 4) * 8 + kc) * 128
                        P.op("pe", lambda e, b=b, o=o, kc=kc, sl=sl: e.matmul(
                            ps[b][:, :], lhsT=sl[:, o:o + 128], rhs=oT[:, kc, :], start=(kc == 0), stop=(kc == NKC - 1)),
                            reads=[sk, ("ns_oT", kc)], writes=[("ps", b)])
                    P.op("dve", lambda e, b=b, mo=mo, tq=tq: e.tensor_tensor(
                        out=self.hT[:, mo, tq], in0=ps[b][:, :], in1=self.hT[:, mo, tq], op=ALU.add),
                        writes=[("ps", b), ("hT", mo, Q)])

    def branch_coef(self, den, rd, coef, gate, h, br, bank):
        P = self.P
        P.op("dve", lambda e: e.tensor_scalar(out=rd[:, br, :].unsqueeze(2), in0=den, scalar1=self.gcol(C_TINY), scalar2=None, op0=ALU.add),
             reads=["consts"], writes=[("ps", bank), "ns_rd"])
        P.op("dve", lambda e: e.reciprocal(out=rd[:, br, :], in_=rd[:, br, :]), reads=["ns_rd"], writes=["ns_rd"])
        P.op("dve", lambda e: e.tensor_tensor(out=coef[:, br, :], in0=rd[:, br, :], in1=gate[:, :, 3 * h + br], op=ALU.mult),
             reads=["ns_rd", "ns_gate"], writes=["ns_coef"])

    def build(self):
        nc, P = self.nc, self.P
        self.prologue()
        for ph in self.phases:
            if ph.startswith("ffn"):
                self.ffn(int(ph[3:]))
            elif ph == "conv":
                self.conv()
            elif ph == "kv":
                self.kv()
            elif ph == "nsa":
                self.nsa()
        self.epilogue()
        with nc.Block() as block:
            sems = {e: nc.alloc_semaphore(f"s_{e}") for e in Prog.ENGS}
            dsems = {q: [nc.alloc_semaphore(f"d{q}_{i}") for i in range(Prog.N_DMA_SEMS)] for q in ("pool", "sp")}
            P.emit(block, sems, dsems)
        return nc


def _prep_inputs(inputs, phases):
    f = lambda a: np.ascontiguousarray(np.asarray(a, np.float32))
    m = {}
    consts = np.zeros((128, NCONST), np.float32)
    consts[:, 0:32] = _fm(f(inputs["ffn_norm"]).reshape(4, D))
    consts[:, 32:48] = _fm(f(inputs["mix_norm"]))
    consts[:, 48:56] = _fm(f(inputs["kv_norm"]).reshape(1, D))
    consts[:, C_CONVW:C_CONVW + 24] = _fm(f(inputs["conv_w"]).reshape(3, D))
    kn = f(inputs["k_norm"])
    consts[0:64, C_DK:C_DK + 3] = kn.T
    consts[64:128, C_DK:C_DK + 3] = kn.T
    consts[0:64, C_DK + 3] = f(inputs["q_norm"])[0]
    consts[64:128, C_DK + 3] = f(inputs["q_norm"])[0]
    b1 = f(inputs["cmp_b1"])
    consts[:, C_B1:C_B1 + 4] = b1.reshape(2, 2, 128).transpose(2, 0, 1).reshape(128, 4)
    consts[:, C_EPS] = EPS
    m["consts"] = consts
    gu = f(inputs["ffn_w_gate_up"]).reshape(4, D, 2 * DFF)
    dn = f(inputs["ffn_w_down"]).reshape(4, DFF, D)
    m["w_gu"] = np.stack([_img_gu(gu[i]) for i in range(4)])
    m["w_dn"] = np.stack([_img_dn(dn[i]) for i in range(4)])
    m["w_cin"] = _img_cin(f(inputs["conv_w_in"])[0])
    m["w_cout"] = _img_sq(f(inputs["conv_w_out"])[0])
    kvw = f(inputs["kv_w"]).reshape(8, 128, 6, 4, 64)
    ra = kvw[:, :, [0, 1]]
    ra = ra.transpose(1, 3, 0, 2, 4).reshape(128, 4096)
    kb = kvw[:, :, [2, 4]]
    kb = kb.transpose(1, 2, 3, 0, 4).reshape(128, 4096)
    m["w_kvk"] = np.ascontiguousarray(np.stack([ra, kb]))
    vv = kvw[:, :, [3, 5]]
    m["w_kvv"] = np.ascontiguousarray(vv.transpose(1, 0, 2, 3, 4)).reshape(128, 4096)
    w1 = f(inputs["cmp_w1"]).reshape(2, 2, 16, 64, 256)
    m["w_c1"] = np.ascontiguousarray(w1.transpose(0, 1, 3, 2, 4)).reshape(2, 2, 64, 4096)
    w2 = f(inputs["cmp_w2"]).reshape(2, 2, 128, 64)
    m["w_c2"] = np.ascontiguousarray(w2.transpose(2, 0, 1, 3)).reshape(128, 256)
    pos = f(inputs["cmp_pos"])
    m["posT"] = np.ascontiguousarray(pos.transpose(0, 2, 1)).reshape(128, 32)
    wqg = f(inputs["nsa_w_qg"])[0]
    wq = wqg[:, :1024].reshape(8, 128, 2, 8, 64)
    m["w_q"] = np.ascontiguousarray(wq.transpose(2, 1, 3, 0, 4)).reshape(2, 128, 4096)
    wg = wqg[:, 1024:].reshape(8, 128, 48)
    m["w_g"] = np.ascontiguousarray(wg.transpose(1, 0, 2)).reshape(128, 384)
    m["w_o"] = _img_sq(f(inputs["nsa_w_o"])[0])
    m.update(_mask_consts())
    m["consts"][:, C_MUL:C_MUL + 63] = m.pop("_mul")
    m["consts"][:, C_ADD:C_ADD + 63] = m.pop("_add")
    m["consts"][:, C_TINY] = 1e-30
    return m


_MASKS = {}


def _mask_consts():
    if _MASKS:
        return dict(_MASKS)
    key = np.arange(S)
    eind = (key[None, :] // 64 == np.arange(32)[:, None]).astype(np.float32)
    n = np.arange(128)
    valid = ((16 * n[:, None] + 31 <= key[None, :]) & (n[:, None] < 127)).astype(np.float32)
    p = np.arange(128)
    tri = np.concatenate([(p[:, None] <= p[None, :]), (p[:, None] > p[None, :]), (p[:, None] == p[None, :])], axis=1).astype(np.float32)
    j = np.arange(32)
    ovl = ((16 * n[:, None] < 64 * j[None, :] + 64) & (16 * n[:, None] + 32 > 64 * j[None, :]) & (n[:, None] < 127)).astype(np.float32)
    c = (p >= 64).astype(np.int64)[:, None]
    jr = (np.arange(63) - 31)[None, :]
    forced0 = jr == c
    forced1 = jr == c - 1
    future = jr > c
    mul = (~(forced0 | forced1 | future)).astype(np.float32)
    add = np.where(forced0, 2e9, np.where(forced1, 1e9, np.where(future, -1e30, 0.0))).astype(np.float32)
    _MASKS.update({"m_eind": eind, "m_valid": valid, "m_tri": tri, "m_ovl": ovl, "_mul": mul, "_add": add})
    return dict(_MASKS)


_NC_CACHE = {}


def run_phases(inputs, phases, xT_list):
    key = tuple(phases)
    if key not in _NC_CACHE:
        _NC_CACHE[key] = Builder(phases).build()
    nc = _NC_CACHE[key]
    shared = _prep_inputs(inputs, phases)
    in_maps = []
    for xT in xT_list:
        d = dict(shared)
        d["xT"] = np.ascontiguousarray(xT, np.float32)
        in_maps.append(d)
    res = run_bass_kernel_spmd(nc, in_maps, core_ids=list(range(len(xT_list))))
    return [r["yT"] for r in res.results]


def kernel(**inputs):
    x = np.asarray(inputs["x"], np.float32)
    xT = [np.ascontiguousarray(x[b].T) for b in range(x.shape[0])]
    yT = run_phases(inputs, ALL_PHASES, xT)
    return np.stack([np.ascontiguousarray(y.T) for y in yT]).astype(np.float32)
```

```python
from contextlib import ExitStack

import numpy as np
import concourse.bass as bass
import concourse.mybir as mybir
from concourse.bass_utils import run_bass_kernel_spmd

F32 = mybir.dt.float32
BF16 = mybir.dt.bfloat16
AF = mybir.ActivationFunctionType
ALU = mybir.AluOpType
AX = mybir.AxisListType


class Node:
    __slots__ = ("eng", "fn", "deps", "signal", "sem", "val", "dma", "seq")

    def __init__(self, eng, fn, dma):
        self.eng = eng
        self.fn = fn
        self.dma = dma
        self.deps = []
        self.signal = False
        self.sem = None
        self.val = 0
        self.seq = 0


class Prog:
    ENGS = ("pe", "act", "dve", "pool", "sp")
    N_DMA_SEMS = 16

    def __init__(self, nc):
        self.nc = nc
        self.nodes = {e: [] for e in self.ENGS}
        self.last_w = {}
        self.readers = {}
        self.dma_nodes = {"pool": [], "sp": [], "act": []}
        self.fence = []
        self.fenced = set()
        self.out_dmas = []

    def new_phase(self):
        self.fence = [self.nodes[e][-1] for e in self.ENGS if self.nodes[e]]
        self.fenced = set()

    def op(self, eng, fn, reads=(), writes=(), dma=False, arena=True, out=False):
        n = Node(eng, fn, dma)
        deps = []
        for k in reads:
            w = self.last_w.get(k)
            if w is not None:
                deps.append(w)
        for k in writes:
            w = self.last_w.get(k)
            if w is not None:
                deps.append(w)
            r = self.readers.get(k)
            if r:
                deps.extend(r.values())
        if arena and eng not in self.fenced:
            deps.extend(self.fence)
            self.fenced.add(eng)
        if dma:
            lst = self.dma_nodes[eng]
            i = len(lst)
            if i >= self.N_DMA_SEMS:
                deps.append(lst[i - self.N_DMA_SEMS])
            lst.append(n)
            if out:
                self.out_dmas.append(n)
        for k in reads:
            rk = id(n) if dma else eng
            self.readers.setdefault(k, {})[rk] = n
        for k in writes:
            self.last_w[k] = n
            self.readers[k] = {}
        seen = set()
        for d in deps:
            if d is n or id(d) in seen:
                continue
            seen.add(id(d))
            if d.eng == "pe" and eng == "pe" and not d.dma and not dma:
                continue
            n.deps.append(d)
            d.signal = True
        n.seq = len(self.nodes[eng])
        self.nodes[eng].append(n)
        return n

    def emit(self, block, sems, dma_sems):
        nc = self.nc
        for e in self.ENGS:
            c = 0
            for n in self.nodes[e]:
                if n.dma:
                    continue
                if n.signal:
                    c += 1
                    n.sem = sems[e]
                    n.val = c
        for q, lst in self.dma_nodes.items():
            if not lst:
                continue
            qs = dma_sems[q]
            cnt = [0] * len(qs)
            for i, n in enumerate(lst):
                s = i % len(qs)
                cnt[s] += 16
                n.sem = qs[s]
                n.val = cnt[s]
                n.signal = True
        out_dmas = self.out_dmas

        def run(e, engine):
            waited = {}
            for n in self.nodes[e]:
                need = {}
                for d in n.deps:
                    k = d.sem.num
                    if need.get(k, (0, None))[0] < d.val:
                        need[k] = (d.val, d.sem)
                for k, (v, s) in need.items():
                    if waited.get(k, 0) < v:
                        engine.wait_ge(s, v)
                        waited[k] = v
                ins = n.fn(engine)
                if n.signal:
                    ins.then_inc(n.sem, 16 if n.dma else 1)
            if e == "sp":
                for d in out_dmas:
                    if waited.get(d.sem.num, 0) < d.val:
                        engine.wait_ge(d.sem, d.val)
                        waited[d.sem.num] = d.val

        block.tensor(lambda eng: run("pe", eng))
        block.scalar(lambda eng: run("act", eng))
        block.vector(lambda eng: run("dve", eng))
        block.gpsimd(lambda eng: run("pool", eng))
        block.sync(lambda eng: run("sp", eng))


D = 1024
S = 2048
DFF = 2816
NKC = 8
NJ = 22
NS = 5
SLOT = 4096
EPS = 1e-6
NCONST = 224
C_GAIN = 0
C_CONVW = 56
C_DK = 80
C_B1 = 84
C_EPS = 88
C_TINY = 89
C_MUL = 96
C_ADD = 160
KSLOTS = (0, 1, 2, 4)
GELU_C = 1.5957691216057308
ALL_PHASES = ("ffn0", "conv", "ffn1", "kv", "ffn2", "nsa", "ffn3")


def _img_gu(w):
    a = w.reshape(8, 128, 2, 11, 2, 128)
    return np.ascontiguousarray(a.transpose(3, 1, 4, 2, 0, 5)).reshape(11, 128, 4096)


def _img_dn(w):
    a = w.reshape(2, 11, 128, 4, 2, 128)
    return np.ascontiguousarray(a.transpose(0, 3, 2, 4, 1, 5)).reshape(2, 4, 128, 2816)


def _img_cin(w):
    a = w.reshape(8, 128, 3, 8, 128)
    return np.ascontiguousarray(a.transpose(3, 1, 2, 0, 4)).reshape(8, 128, 3072)


def _img_sq(w):
    a = w.reshape(8, 128, 2, 4, 128)
    return np.ascontiguousarray(a.transpose(2, 1, 3, 0, 4)).reshape(2, 128, 4096)


def _fm(v):
    v = np.asarray(v, np.float32).reshape(-1, 8, 128)
    return np.ascontiguousarray(v.transpose(2, 0, 1)).reshape(128, -1)


class Builder:
    def __init__(self, phases, debug=False):
        self.phases = tuple(phases)
        self.debug = debug
        nc = self.nc = bass.Bass("TRN2", target_bir_lowering=False)
        self.P = Prog(nc)
        self.bank_rr = 0
        self.slot_rr = 0
        self.uid = 0
        dt = nc.dram_tensor
        self.d_x = dt("xT", [D, S], F32, kind="ExternalInput").ap()
        self.d_y = dt("yT", [D, S], F32, kind="ExternalOutput").ap()
        self.d_consts = dt("consts", [128, NCONST], F32, kind="ExternalInput").ap()
        self.d_wgu = dt("w_gu", [4, 11, 128, 4096], F32, kind="ExternalInput").ap()
        self.d_wdn = dt("w_dn", [4, 2, 4, 128, 2816], F32, kind="ExternalInput").ap()
        self.d_cin = dt("w_cin", [8, 128, 3072], F32, kind="ExternalInput").ap()
        self.d_cout = dt("w_cout", [2, 128, 4096], F32, kind="ExternalInput").ap()
        self.d_kvk = dt("w_kvk", [2, 128, 4096], F32, kind="ExternalInput").ap()
        self.d_kvv = dt("w_kvv", [128, 4096], F32, kind="ExternalInput").ap()
        self.d_w1 = dt("w_c1", [2, 2, 64, 4096], F32, kind="ExternalInput").ap()
        self.d_w2 = dt("w_c2", [128, 256], F32, kind="ExternalInput").ap()
        self.d_pos = dt("posT", [128, 32], F32, kind="ExternalInput").ap()
        self.d_eind = dt("m_eind", [32, S], F32, kind="ExternalInput").ap()
        self.d_valid = dt("m_valid", [128, S], F32, kind="ExternalInput").ap()
        self.d_tri = dt("m_tri", [128, 384], F32, kind="ExternalInput").ap()
        self.d_ovl = dt("m_ovl", [128, 32], F32, kind="ExternalInput").ap()
        self.d_wq = dt("w_q", [2, 128, 4096], F32, kind="ExternalInput").ap()
        self.d_wg = dt("w_g", [128, 384], F32, kind="ExternalInput").ap()
        self.d_wo = dt("w_o", [2, 128, 4096], F32, kind="ExternalInput").ap()
        A = nc.alloc_sbuf_tensor
        self.hT = A("hT", [128, NKC, S], F32)
        self.slots = [A(f"wslot{i}", [128, SLOT], BF16) for i in range(NS)]
        self.consts = A("consts_sb", [128, NCONST], F32)
        self.ones = A("ones_bf", [128, 128], BF16)
        self.ps = [nc.alloc_psum_tensor(f"psb{i}", [128, 512], F32) for i in range(8)]

    def key(self, name):
        self.uid += 1
        return (name, self.uid)

    def bank(self):
        b = self.bank_rr
        self.bank_rr = (b + 1) % 8
        return b

    def load_w(self, src_ap, nelem, parts=128, p0=0):
        i = self.slot_rr
        self.slot_rr = (i + 1) % NS
        sl = self.slots[i]
        self.P.op("pool", lambda e: e.dma_start(out=sl[p0:p0 + parts, 0:nelem], in_=src_ap),
                  writes=[("slot", i)], dma=True, arena=False)
        return sl, ("slot", i)

    def gcol(self, c):
        return self.consts[:, c:c + 1]

    def prologue(self):
        P = self.P
        P.op("sp", lambda e: e.dma_start(out=self.consts[:, :], in_=self.d_consts), writes=["consts"], dma=True, arena=False)
        P.op("dve", lambda e: e.memset(self.ones[:, :], 1.0), writes=["ones"], arena=False)
        xv = self.d_x.rearrange("(c p) t -> p c t", p=128)
        for c in range(NKC):
            P.op("sp", lambda e, c=c: e.dma_start(out=self.hT[:, c, :], in_=xv[:, c, :]),
                 writes=[("hT", c, q) for q in range(4)], dma=True, arena=False)

    def epilogue(self):
        P = self.P
        yv = self.d_y.rearrange("(c p) t -> p c t", p=128)
        for c in range(NKC):
            P.op("sp", lambda e, c=c: e.dma_start(out=yv[:, c, :], in_=self.hT[:, c, :]),
                 reads=[("hT", c, q) for q in range(4)], dma=True, arena=False, out=True)

    def rmsnorm_tile(self, q, gain_col, xn, xn_off, xn_key, work, bank=None, lnexp=True):
        P = self.P
        sq, rstd = work
        tok = slice(q * 512, (q + 1) * 512)
        b = self.bank() if bank is None else bank
        for c in range(NKC):
            s = sq[c % 2]
            sk = ("sq", c % 2)
            P.op("act", lambda e, c=c, s=s: e.activation(out=s[:, :], in_=self.hT[:, c, tok], func=AF.Square),
                 reads=[("hT", c, q)], writes=[sk])
            P.op("pe", lambda e, c=c, s=s: e.matmul(self.ps[b][:, :], lhsT=self.ones[:, :], rhs=s[:, :],
                                                   start=(c == 0), stop=(c == NKC - 1)),
                 reads=[sk, "ones"], writes=[("ps", b)])
        if lnexp:
            P.op("act", lambda e: e.activation(out=rstd[:, :], in_=self.ps[b][:, :], func=AF.Ln,
                                               bias=self.gcol(C_EPS), scale=1.0 / D),
                 reads=["consts"], writes=[("ps", b), "rstd"])
            P.op("act", lambda e: e.activation(out=rstd[:, :], in_=rstd[:, :], func=AF.Exp, scale=-0.5),
                 reads=["rstd"], writes=["rstd"])
        else:
            P.op("act", lambda e: e.activation(out=rstd[:, :], in_=self.ps[b][:, :], func=AF.Sqrt,
                                               bias=self.gcol(C_EPS), scale=1.0 / D),
                 reads=["consts"], writes=[("ps", b), "rstd"])
            P.op("dve", lambda e: e.reciprocal(out=rstd[:, :], in_=rstd[:, :]), reads=["rstd"], writes=["rstd"])
        for c in range(NKC):
            P.op("dve", lambda e, c=c: e.scalar_tensor_tensor(
                out=xn[:, c, xn_off:xn_off + 512], in0=self.hT[:, c, tok], scalar=self.gcol(gain_col + c),
                in1=rstd[:, :], op0=ALU.mult, op1=ALU.mult),
                reads=[("hT", c, q), "rstd", "consts"], writes=[(xn_key, c, xn_off // 512)])

    def ffn(self, li):
        nc, P = self.nc, self.P
        P.new_phase()
        with (
            nc.sbuf_tensor(f"ffn_xn{li}", [128, NKC, 1024], BF16) as xn,
            nc.sbuf_tensor(f"ffn_act{li}", [128, 11, 1024], BF16) as act,
            nc.sbuf_tensor(f"ffn_sq0{li}", [128, 512], BF16) as sq0,
            nc.sbuf_tensor(f"ffn_sq1{li}", [128, 512], BF16) as sq1,
            nc.sbuf_tensor(f"ffn_rstd{li}", [128, 512], F32) as rstd,
            nc.sbuf_tensor(f"ffn_sa0{li}", [128, 512], F32) as sa0,
            nc.sbuf_tensor(f"ffn_sa1{li}", [128, 512], F32) as sa1,
            nc.sbuf_tensor(f"ffn_sa2{li}", [128, 512], F32) as sa2,
        ):
            sas = [sa0, sa1, sa2]
            sa_rr = 0
            for half in range(2):
                for tt in range(2):
                    self.rmsnorm_tile(half * 2 + tt, C_GAIN + li * 8, xn, tt * 512, "ffn_xn", ((sq0, sq1), rstd))
                for jh in range(2):
                    jlist = list(range(jh * 11, jh * 11 + 11))
                    loaded = {}
                    for j in jlist:
                        jp, jj = divmod(j, 2)
                        if jp not in loaded:
                            loaded[jp] = self.load_w(self.d_wgu[li, jp], 4096)
                        sl, sk = loaded[jp]
                        jl = j - jh * 11
                        for tt in range(2):
                            ba, bb = self.bank(), self.bank()
                            tk = slice(tt * 512, (tt + 1) * 512)
                            for ab, b in ((0, ba), (1, bb)):
                                for kc in range(NKC):
                                    o = ((jj * 2 + ab) * 8 + kc) * 128
                                    P.op("pe", lambda e, b=b, o=o, kc=kc, sl=sl, tk=tk: e.matmul(
                                        self.ps[b][:, :], lhsT=sl[:, o:o + 128], rhs=xn[:, kc, tk],
                                        start=(kc == 0), stop=(kc == NKC - 1)),
                                        reads=[sk, ("ffn_xn", kc, tt)], writes=[("ps", b)])
                            sa = sas[sa_rr]
                            sak = ("ffn_sa", sa_rr)
                            sa_rr = (sa_rr + 1) % 3
                            P.op("act", lambda e, sa=sa, ba=ba: e.activation(out=sa[:, :], in_=self.ps[ba][:, :], func=AF.Silu),
                                 writes=[("ps", ba), sak])
                            P.op("dve", lambda e, sa=sa, bb=bb, jl=jl, tk=tk: e.tensor_tensor(
                                out=act[:, jl, tk], in0=sa[:, :], in1=self.ps[bb][:, :], op=ALU.mult),
                                reads=[sak], writes=[("ps", bb), ("ffn_act", jl, tt)])
                    for mp in range(4):
                        sl, sk = self.load_w(self.d_wdn[li, jh, mp], 2816)
                        for ml in range(2):
                            m = mp * 2 + ml
                            for tt in range(2):
                                b = self.bank()
                                q = half * 2 + tt
                                tk = slice(tt * 512, (tt + 1) * 512)
                                for jl in range(11):
                                    o = (ml * 11 + jl) * 128
                                    P.op("pe", lambda e, b=b, o=o, jl=jl, sl=sl, tk=tk: e.matmul(
                                        self.ps[b][:, :], lhsT=sl[:, o:o + 128], rhs=act[:, jl, tk],
                                        start=(jl == 0), stop=(jl == 10)),
                                        reads=[sk, ("ffn_act", jl, tt)], writes=[("ps", b)])
                                P.op("dve", lambda e, b=b, m=m, q=q: e.scalar_tensor_tensor(
                                    out=self.hT[:, m, q * 512:(q + 1) * 512], in0=self.ps[b][:, :], scalar=0.5,
                                    in1=self.hT[:, m, q * 512:(q + 1) * 512], op0=ALU.mult, op1=ALU.add),
                                    reads=[], writes=[("ps", b), ("hT", m, q)])

    def conv(self):
        nc, P = self.nc, self.P
        P.new_phase()
        with (
            nc.sbuf_tensor("cv_xn", [128, NKC, S], BF16) as xn,
            nc.sbuf_tensor("cv_gated", [128, NKC, S], BF16) as gated,
            nc.sbuf_tensor("cv_sq0", [128, 512], BF16) as sq0,
            nc.sbuf_tensor("cv_sq1", [128, 512], BF16) as sq1,
            nc.sbuf_tensor("cv_rstd", [128, 512], F32) as rstd,
            nc.sbuf_tensor("cv_csb", [128, 512], F32) as csb,
            nc.sbuf_tensor("cv_v", [128, 2, 514], F32) as vbuf,
            nc.sbuf_tensor("cv_acc", [128, 2, 512], F32) as accb,
        ):
            for q in range(4):
                self.rmsnorm_tile(q, C_GAIN + 32, xn, q * 512, "cv_xn", ((sq0, sq1), rstd))
            vi = 0
            for m in range(NKC):
                sl, sk = self.load_w(self.d_cin[m], 3072)
                for q in range(4):
                    tk = slice(q * 512, (q + 1) * 512)
                    bs = [self.bank(), self.bank(), self.bank()]
                    for s3 in range(3):
                        for kc in range(NKC):
                            o = (s3 * 8 + kc) * 128
                            P.op("pe", lambda e, b=bs[s3], o=o, kc=kc, sl=sl, tk=tk: e.matmul(
                                self.ps[b][:, :], lhsT=sl[:, o:o + 128], rhs=xn[:, kc, tk],
                                start=(kc == 0), stop=(kc == NKC - 1)),
                                reads=[sk, ("cv_xn", kc, q)], writes=[("ps", bs[s3])])
                    v = vbuf[:, vi, :]
                    vk = ("cv_v", vi)
                    vprev = vbuf[:, 1 - vi, :]
                    vpk = ("cv_v", 1 - vi)
                    acc = accb[:, vi, :]
                    ak = ("cv_acc", vi)
                    vi = 1 - vi
                    if q == 0:
                        P.op("dve", lambda e, v=v: e.memset(v[:, 0:2], 0.0), writes=[vk])
                    else:
                        P.op("dve", lambda e, v=v, vprev=vprev: e.tensor_copy(out=v[:, 0:2], in_=vprev[:, 512:514]),
                             reads=[vpk], writes=[vk])
                    P.op("act", lambda e, b=bs[1]: e.activation(out=csb[:, :], in_=self.ps[b][:, :], func=AF.Copy),
                         writes=[("ps", bs[1]), "cv_csb"])
                    P.op("dve", lambda e, v=v, b=bs[2]: e.tensor_tensor(out=v[:, 2:514], in0=csb[:, :], in1=self.ps[b][:, :], op=ALU.mult),
                         reads=["cv_csb"], writes=[("ps", bs[2]), vk])
                    P.op("dve", lambda e, v=v, acc=acc, m=m: e.tensor_scalar(
                        out=acc, in0=v[:, 0:512], scalar1=self.gcol(C_CONVW + 0 * 8 + m), scalar2=None, op0=ALU.mult),
                        reads=[vk, "consts"], writes=[ak])
                    for k in (1, 2):
                        P.op("dve", lambda e, v=v, acc=acc, m=m, k=k: e.scalar_tensor_tensor(
                            out=acc, in0=v[:, k:k + 512], scalar=self.gcol(C_CONVW + k * 8 + m), in1=acc,
                            op0=ALU.mult, op1=ALU.add),
                            reads=[vk, ak, "consts"], writes=[ak])
                    P.op("dve", lambda e, acc=acc, b=bs[0], m=m, tk=tk: e.tensor_tensor(
                        out=gated[:, m, tk], in0=acc, in1=self.ps[b][:, :], op=ALU.mult),
                        reads=[ak], writes=[("ps", bs[0]), ("cv_gated", m, q)])
            for mh in range(2):
                sl, sk = self.load_w(self.d_cout[mh], 4096)
                for ml in range(4):
                    mo = mh * 4 + ml
                    for q in range(4):
                        tk = slice(q * 512, (q + 1) * 512)
                        b = self.bank()
                        for kc in range(NKC):
                            o = (ml * 8 + kc) * 128
                            P.op("pe", lambda e, b=b, o=o, kc=kc, sl=sl, tk=tk: e.matmul(
                                self.ps[b][:, :], lhsT=sl[:, o:o + 128], rhs=gated[:, kc, tk],
                                start=(kc == 0), stop=(kc == NKC - 1)),
                                reads=[sk, ("cv_gated", kc, q)], writes=[("ps", b)])
                        P.op("dve", lambda e, b=b, mo=mo, tk=tk: e.tensor_tensor(
                            out=self.hT[:, mo, tk], in0=self.ps[b][:, :], in1=self.hT[:, mo, tk], op=ALU.add),
                            writes=[("ps", b), ("hT", mo, q)])

    def dknorm(self, b, n, gain_col, dest, dest_keys, work, b2):
        P = self.P
        sqd, rs = work
        src = self.ps[b][0:64, 0:n]
        P.op("act", lambda e: e.activation(out=sqd[0:64, 0:n], in_=src, func=AF.Square),
             writes=[("ps", b), "dk_sq"])
        P.op("pe", lambda e: e.matmul(self.ps[b2][0:64, 0:n], lhsT=self.ones[0:64, 0:64], rhs=sqd[0:64, 0:n],
                                      start=True, stop=True),
             reads=["dk_sq", "ones"], writes=[("ps", b2)])
        P.op("act", lambda e: e.activation(out=rs[0:64, 0:n], in_=self.ps[b2][0:64, 0:n], func=AF.Ln,
                                           bias=self.consts[0:64, C_EPS:C_EPS + 1], scale=1.0 / 64),
             reads=["consts"], writes=[("ps", b2), "dk_rs"])
        P.op("act", lambda e: e.activation(out=rs[0:64, 0:n], in_=rs[0:64, 0:n], func=AF.Exp, scale=-0.5),
             reads=["dk_rs"], writes=["dk_rs"])
        rsv = rs[0:64, 0:n]
        srcv = src
        if len(dest.shape) == 3:
            g_ = dest.shape[1]
            rsv = rsv.rearrange("p (g n) -> p g n", g=g_)
            srcv = srcv.rearrange("p (g n) -> p g n", g=g_)
        P.op("dve", lambda e: e.scalar_tensor_tensor(out=dest, in0=srcv, scalar=self.consts[0:64, gain_col:gain_col + 1],
                                                     in1=rsv, op0=ALU.mult, op1=ALU.mult),
             reads=["dk_rs", "consts"], writes=[("ps", b)] + list(dest_keys))

    def kv(self):
        nc, P = self.nc, self.P
        P.new_phase()
        A = nc.alloc_sbuf_tensor
        self.kselT = A("kselT", [96, 4, S], BF16)
        self.kwinT = A("kwinT", [64, 4, S], BF16)
        self.vsel = A("vsel", [128, 16, 4, 65], BF16)
        self.vwin = A("vwin", [128, 16, 4, 65], BF16)
        self.kcmpT = A("kcmpT", [64, 4, 128], BF16)
        self.vcmp = A("vcmp", [128, 4, 97], BF16)
        kselT, kwinT, vsel, vwin, kcmpT, vcmp = self.kselT, self.kwinT, self.vsel, self.vwin, self.kcmpT, self.vcmp
        with (
            nc.sbuf_tensor("kv_xn", [128, NKC, 512], BF16) as xn,
            nc.sbuf_tensor("kv_raw", [128, 4, 16, 128], BF16) as rawT,
            nc.sbuf_tensor("kv_sq0", [128, 512], BF16) as sq0,
            nc.sbuf_tensor("kv_sq1", [128, 512], BF16) as sq1,
            nc.sbuf_tensor("kv_rstd", [128, 512], F32) as rstd,
            nc.sbuf_tensor("kv_sqd", [64, 512], BF16) as sqd,
            nc.sbuf_tensor("kv_rs", [64, 512], F32) as rs,
            nc.sbuf_tensor("kv_pos", [128, 32], BF16) as posT,
            nc.sbuf_tensor("kv_w2", [128, 256], BF16) as w2t,
            nc.sbuf_tensor("kv_u", [128, 508], F32) as ubuf,
            nc.sbuf_tensor("kv_t", [128, 508], F32) as tbuf,
            nc.sbuf_tensor("kv_cb", [128, 1], F32) as cb,
            nc.sbuf_tensor("kv_h1", [128, 2, 508], BF16) as h1,
        ):
            for g in range(4):
                P.op("pool", lambda e, g=g: e.dma_start(out=kselT[64:96, g, :], in_=self.d_eind), writes=[("kselE", g)], dma=True)
                P.op("pool", lambda e, g=g: e.dma_start(out=vcmp[:, g, 65:97], in_=self.d_ovl), writes=[("vcmp_o", g)], dma=True)
            P.op("pool", lambda e: e.dma_start(out=posT[:, :], in_=self.d_pos), writes=["kv_pos"], dma=True)
            P.op("pool", lambda e: e.dma_start(out=w2t[:, :], in_=self.d_w2), writes=["kv_w2"], dma=True)
            P.op("dve", lambda e: e.memset(kcmpT[:, :, :], 0.0), writes=["kcmpT"])
            P.op("dve", lambda e: e.memset(vcmp[:, :, 0:64], 0.0), writes=[("vcmp_v", g) for g in range(4)])
            P.op("dve", lambda e: e.memset(vcmp[:, :, 64:65], 1.0), writes=["vcmp_1"])
            P.op("dve", lambda e: e.memset(vsel[:, :, :, 64:65], 1.0), writes=["vsel_1"])
            P.op("dve", lambda e: e.memset(vwin[:, :, :, 64:65], 1.0), writes=["vwin_1"])
            slk = [self.load_w(self.d_kvk[i], 4096) for i in range(2)]
            slv, slvk = self.load_w(self.d_kvv, 4096)
            for q in range(4):
                tk = slice(q * 512, (q + 1) * 512)
                self.rmsnorm_tile(q, C_GAIN + 48, xn, 0, "kv_xn", ((sq0, sq1), rstd))
                slA, skA = slk[0]
                for g in range(4):
                    b = self.bank()
                    for kc in range(NKC):
                        o = (g * 8 + kc) * 128
                        P.op("pe", lambda e, b=b, o=o, kc=kc: e.matmul(
                            self.ps[b][:, :], lhsT=slA[:, o:o + 128], rhs=xn[:, kc, :],
                            start=(kc == 0), stop=(kc == NKC - 1)),
                            reads=[skA, ("kv_xn", kc, 0)], writes=[("ps", b)])
                    P.op("act", lambda e, b=b, g=g, q=q: e.activation(
                        out=rawT[:, g, :, q * 32:(q + 1) * 32], in_=self.ps[b][:, :].rearrange("p (n l) -> p l n", l=16), func=AF.Copy),
                        writes=[("ps", b), ("kv_raw", g, q)])
                slB, skB = slk[1]
                for blk in range(8):
                    si, g = divmod(blk, 4)
                    b = self.bank()
                    for kc in range(NKC):
                        o = (blk * 8 + kc) * 64
                        P.op("pe", lambda e, b=b, o=o, kc=kc: e.matmul(
                            self.ps[b][0:64, :], lhsT=slB[:, o:o + 64], rhs=xn[:, kc, :],
                            start=(kc == 0), stop=(kc == NKC - 1)),
                            reads=[skB, ("kv_xn", kc, 0)], writes=[("ps", b)])
                    if si == 0:
                        self.dknorm(b, 512, C_DK + 1, kselT[0:64, g, tk], [("kselT", g, q)], (sqd, rs), self.bank())
                    else:
                        self.dknorm(b, 512, C_DK + 2, kwinT[0:64, g, tk], [("kwinT", g, q)], (sqd, rs), self.bank())
                for sub in range(4):
                    tt = q * 4 + sub
                    b = self.bank()
                    for kc in range(NKC):
                        P.op("pe", lambda e, b=b, kc=kc, sub=sub: e.matmul(
                            self.ps[b][:, :], lhsT=xn[:, kc, sub * 128:(sub + 1) * 128], rhs=slv[:, kc * 512:(kc + 1) * 512],
                            start=(kc == 0), stop=(kc == NKC - 1)),
                            reads=[slvk, ("kv_xn", kc, 0)], writes=[("ps", b)])
                    P.op("act", lambda e, b=b, tt=tt: e.activation(
                        out=vsel[:, tt, :, 0:64], in_=self.ps[b][:, 0:256].rearrange("p (g d) -> p g d", g=4), func=AF.Copy),
                        writes=[("ps", b), ("vsel", tt)])
                    P.op("dve", lambda e, b=b, tt=tt: e.tensor_copy(
                        out=vwin[:, tt, :, 0:64], in_=self.ps[b][:, 256:512].rearrange("p (g d) -> p g d", g=4)),
                        writes=[("ps", b), ("vwin", tt)])
            raw_keys = lambda s_: [("kv_raw", g, q) for g in range(4) for q in range(4)]
            for s_ in range(2):
                pl = slice(s_ * 64, s_ * 64 + 64)
                w1s = [self.load_w(self.d_w1[s_, hf], 4096, parts=64, p0=s_ * 64) for hf in range(2)]
                for hc in range(2):
                    b = self.bank()
                    b1 = self.bank()
                    for l in range(32):
                        sl, sk = w1s[l // 16]
                        o = (l % 16) * 256 + hc * 128
                        P.op("pe", lambda e, b=b, sl=sl, o=o, l=l, pl=pl: e.matmul(
                            self.ps[b][:, 0:508], lhsT=sl[pl, o:o + 128],
                            rhs=rawT[pl, :, l % 16, (l // 16):(l // 16) + 127], start=(l == 0), stop=(l == 31)),
                            reads=[sk] + raw_keys(s_), writes=[("ps", b)])
                    for l in range(32):
                        sl, sk = w1s[l // 16]
                        o = (l % 16) * 256 + hc * 128
                        P.op("pe", lambda e, b1=b1, sl=sl, o=o, l=l, pl=pl: e.matmul(
                            self.ps[b1][:, 0:1], lhsT=sl[pl, o:o + 128],
                            rhs=posT[pl, l:l + 1], start=(l == 0), stop=(l == 31)),
                            reads=[sk, "kv_pos"], writes=[("ps", b1)])
                    P.op("dve", lambda e, b1=b1, s_=s_, hc=hc: e.tensor_scalar(
                        out=cb[:, :], in0=self.ps[b1][:, 0:1], scalar1=self.gcol(C_B1 + s_ * 2 + hc), scalar2=None, op0=ALU.add),
                        reads=["consts"], writes=[("ps", b1), "kv_cb"])
                    P.op("dve", lambda e, b=b: e.tensor_scalar(
                        out=ubuf[:, :], in0=self.ps[b][:, 0:508], scalar1=cb[:, 0:1], scalar2=None, op0=ALU.add),
                        reads=["kv_cb"], writes=[("ps", b), "kv_u"])
                    P.op("dve", lambda e: e.tensor_tensor(out=tbuf[:, :], in0=ubuf[:, :], in1=ubuf[:, :], op=ALU.mult),
                         reads=["kv_u"], writes=["kv_t"])
                    P.op("dve", lambda e: e.tensor_scalar(out=tbuf[:, :], in0=tbuf[:, :], scalar1=0.044715, scalar2=1.0,
                                                          op0=ALU.mult, op1=ALU.add), reads=["kv_t"], writes=["kv_t"])
                    P.op("dve", lambda e: e.tensor_tensor(out=tbuf[:, :], in0=tbuf[:, :], in1=ubuf[:, :], op=ALU.mult),
                         reads=["kv_t", "kv_u"], writes=["kv_t"])
                    P.op("act", lambda e: e.activation(out=tbuf[:, :], in_=tbuf[:, :], func=AF.Sigmoid, scale=GELU_C),
                         reads=["kv_t"], writes=["kv_t"])
                    P.op("dve", lambda e, hc=hc: e.tensor_tensor(out=h1[:, hc, :], in0=tbuf[:, :], in1=ubuf[:, :], op=ALU.mult),
                         reads=["kv_t", "kv_u"], writes=[("kv_h1", hc)])
                if s_ == 0:
                    b = self.bank()
                    for hc in range(2):
                        P.op("pe", lambda e, b=b, hc=hc: e.matmul(
                            self.ps[b][0:64, 0:508], lhsT=w2t[:, hc * 64:(hc + 1) * 64], rhs=h1[:, hc, :],
                            start=(hc == 0), stop=(hc == 1)),
                            reads=["kv_w2", ("kv_h1", hc)], writes=[("ps", b)])
                    self.dknorm(b, 508, C_DK + 0, kcmpT[0:64, :, 0:127], ["kcmpT"], (sqd, rs), self.bank())
                else:
                    for g in range(4):
                        b = self.bank()
                        for hc in range(2):
                            P.op("pe", lambda e, b=b, hc=hc, g=g: e.matmul(
                                self.ps[b][0:127, 0:64], lhsT=h1[:, hc, g * 127:(g + 1) * 127],
                                rhs=w2t[:, (2 + hc) * 64:(3 + hc) * 64], start=(hc == 0), stop=(hc == 1)),
                                reads=["kv_w2", ("kv_h1", hc)], writes=[("ps", b)])
                        P.op("act", lambda e, b=b, g=g: e.activation(out=vcmp[0:127, g, 0:64], in_=self.ps[b][0:127, 0:64], func=AF.Copy),
                             writes=[("ps", b), ("vcmp_v", g)])
            if self.debug:
                for nm, t in (("kselT", kselT), ("kwinT", kwinT), ("vsel", vsel), ("vwin", vwin), ("kcmpT", kcmpT), ("vcmp", vcmp)):
                    shp = list(t.shape)
                    dd = nc.dram_tensor("dbg_" + nm, shp, BF16, kind="ExternalOutput").ap()
                    idx = tuple(slice(None) for _ in shp)
                    P.op("sp", lambda e, dd=dd, t=t, idx=idx: e.dma_start(out=dd[idx], in_=t[idx]),
                         reads=[k for k in list(P.last_w.keys()) if isinstance(k, (tuple, str))], dma=True, out=True)

    def nsa(self):
        nc, P = self.nc, self.P
        P.new_phase()
        kselT, kwinT, vsel, vwin, kcmpT, vcmp = self.kselT, self.kwinT, self.vsel, self.vwin, self.kcmpT, self.vcmp
        ps = self.ps
        B_S = (0, 1, 2)
        B_CMP, B_SEL, B_WIN, B_A, B_B = 3, 4, 5, 6, 7
        NPT = 6
        with ExitStack() as st:
            T = lambda name, shape, dt_: st.enter_context(nc.sbuf_tensor(name, shape, dt_))
            xn = T("ns_xn", [128, NKC, 512], BF16)
            sqd2 = T("ns_sqd2", [128, 2, 512], BF16)
            rs2 = T("ns_rs2", [128, 2, 512], F32)
            validT = T("ns_valid", [128, S], BF16)
            tri = T("ns_tri", [128, 384], BF16)
            onesbd = T("ns_onesbd", [128, 128], BF16)
            gate = T("ns_gate", [128, 4, 48], F32)
            qaug = T("ns_qaug", [96, 4, 512], BF16)
            ptb = T("ns_pt", [128, NPT, 512], BF16)
            oacc = T("ns_oacc", [128, 4, 4, 64], F32)
            tmpo = T("ns_tmp", [128, 2, 4, 64], F32)
            rd = T("ns_rd", [128, 3, 4], F32)
            coef = T("ns_coef", [128, 3, 4], F32)
            imp = T("ns_imp", [128, 4, 32], F32)
            impt = T("ns_impt", [128, 4, 32], F32)
            m8 = T("ns_m8", [128, 4, 16], F32)
            biasb = T("ns_bias", [128, 4, 96], BF16)
            opair = T("ns_opair", [128, 2, 4, 128], BF16)
            oT = T("ns_oT", [128, NKC, 512], BF16)
            ident = tri[:, 256:384]
            P.op("pool", lambda e: e.dma_start(out=validT[:, :], in_=self.d_valid), writes=["ns_valid"], dma=True)
            P.op("pool", lambda e: e.dma_start(out=tri[:, :], in_=self.d_tri), writes=["ns_tri", "ns_ident"], dma=True)
            P.op("dve", lambda e: e.memset(qaug[:, :, :], 0.0), writes=[("qaug", i) for i in range(4)] + [("qbias", i) for i in range(4)])
            P.op("dve", lambda e: e.memset(biasb[:, :, :], 0.0), writes=["ns_bias"])
            P.op("dve", lambda e: e.memset(onesbd[:, :], 0.0), writes=["ns_onesbd"])
            P.op("dve", lambda e: e.memset(onesbd[0:64, 0:64], 1.0), writes=["ns_onesbd"])
            P.op("dve", lambda e: e.memset(onesbd[64:128, 64:128], 1.0), writes=["ns_onesbd"])
            rr = {"pt": 0, "s": 0, "o": 0}
            deferred = []

            def defer(fn, delay):
                deferred.append([delay, fn])

            def step():
                due = []
                for it in deferred:
                    it[0] -= 1
                for it in list(deferred):
                    if it[0] <= 0:
                        due.append(it)
                        deferred.remove(it)
                for it in due:
                    it[1]()

            def flush():
                while deferred:
                    step()

            def next_s():
                b = B_S[rr["s"]]; rr["s"] = (rr["s"] + 1) % 3
                return b

            def next_pt():
                i = rr["pt"]; rr["pt"] = (i + 1) % NPT
                return ptb[:, i, :], ("ns_pt", i)

            for Q in range(4):
                tq = slice(Q * 512, (Q + 1) * 512)
                self.rmsnorm_tile(Q, C_GAIN + 40, xn, 0, "ns_xn", ((sqd2[:, 0, :], sqd2[:, 1, :]), rs2[:, 0, :]), bank=B_A, lnexp=True)
                slg, skg = self.load_w(self.d_wg, 384)
                for sub in range(4):
                    for kc in range(NKC):
                        P.op("pe", lambda e, kc=kc, sub=sub: e.matmul(
                            ps[B_B][:, sub * 48:(sub + 1) * 48], lhsT=xn[:, kc, sub * 128:(sub + 1) * 128],
                            rhs=slg[:, kc * 48:(kc + 1) * 48], start=(kc == 0), stop=(kc == NKC - 1)),
                            reads=[skg, ("ns_xn", kc, 0)], writes=[("ps", B_B)])
                P.op("act", lambda e: e.activation(out=gate[:, :, :], in_=ps[B_B][:, 0:192].rearrange("p (s c) -> p s c", s=4),
                                                   func=AF.Sigmoid), writes=[("ps", B_B), "ns_gate"])
                slq = [self.load_w(self.d_wq[i], 4096) for i in range(2)]
                for g in range(4):
                    for pr in range(2):
                        pi_ = g * 2 + pr
                        slh, skh = slq[pi_ // 4]
                        bq = (B_A, B_B)[pr]
                        bss = next_s()
                        sqk = ("sq", pr)
                        rsk = "rstd" if pr == 0 else "rs1"
                        for kc in range(NKC):
                            o = ((pi_ % 4) * 8 + kc) * 128
                            P.op("pe", lambda e, o=o, kc=kc, slh=slh, bq=bq: e.matmul(
                                ps[bq][:, :], lhsT=slh[:, o:o + 128], rhs=xn[:, kc, :],
                                start=(kc == 0), stop=(kc == NKC - 1)),
                                reads=[skh, ("ns_xn", kc, 0)], writes=[("ps", bq)])
                        P.op("act", lambda e, bq=bq, pr=pr: e.activation(out=sqd2[:, pr, :], in_=ps[bq][:, :], func=AF.Square),
                             writes=[("ps", bq), sqk])
                        P.op("pe", lambda e, bss=bss, pr=pr: e.matmul(ps[bss][:, :], lhsT=onesbd[:, :], rhs=sqd2[:, pr, :], start=True, stop=True),
                             reads=[sqk, "ns_onesbd"], writes=[("ps", bss)])
                        P.op("act", lambda e, bss=bss, pr=pr: e.activation(out=rs2[:, pr, :], in_=ps[bss][:, :], func=AF.Ln,
                                                                         bias=self.gcol(C_EPS), scale=1.0 / 64),
                             reads=["consts"], writes=[("ps", bss), rsk])
                        P.op("act", lambda e, pr=pr: e.activation(out=rs2[:, pr, :], in_=rs2[:, pr, :], func=AF.Exp, scale=-0.5),
                             reads=[rsk], writes=[rsk])
                        for j in range(2):
                            hl = pr * 2 + j
                            pj = slice(j * 64, (j + 1) * 64)
                            P.op("dve", lambda e, bq=bq, pr=pr, hl=hl, pj=pj: e.scalar_tensor_tensor(
                                out=qaug[0:64, hl, :], in0=ps[bq][pj, :], scalar=self.consts[pj, C_DK + 3:C_DK + 4],
                                in1=rs2[pj, pr, :], op0=ALU.mult, op1=ALU.mult),
                                reads=[rsk, "consts"], writes=[("ps", bq), ("qaug", hl)])
                    for hl in range(4):
                        h = g * 4 + hl
                        bS = next_s()
                        pt, ptk = next_pt()
                        bo = (B_CMP, B_SEL, B_WIN)[hl % 3]
                        P.op("pe", lambda e, bS=bS, g=g, hl=hl: e.matmul(
                            ps[bS][0:127, :], lhsT=kcmpT[0:64, g, 0:127], rhs=qaug[0:64, hl, :], start=True, stop=False),
                            reads=["kcmpT", ("qaug", hl)], writes=[("ps", bS)])
                        P.op("pe", lambda e, bS=bS, tq=tq: e.matmul(
                            ps[bS][0:127, :], lhsT=tri[0:127, 256:383], rhs=validT[0:127, tq], start=False, stop=True),
                            reads=["ns_valid", "ns_ident"], writes=[("ps", bS)])
                        P.op("act", lambda e, bS=bS, pt=pt: e.activation(out=pt[0:127, :], in_=ps[bS][0:127, :], func=AF.Exp, scale=0.125),
                             writes=[("ps", bS), ptk])

                        def cmp_tail(hl=hl, h=h, pt=pt, ptk=ptk, bo=bo, g=g, Q=Q):
                            for sub in range(4):
                                P.op("pe", lambda e, sub=sub: e.matmul(
                                    ps[bo][:, sub * 97:(sub + 1) * 97], lhsT=pt[0:127, sub * 128:(sub + 1) * 128],
                                    rhs=vcmp[0:127, g, :], start=True, stop=True),
                                    reads=[ptk, ("vcmp_v", g), ("vcmp_o", g), "vcmp_1"], writes=[("ps", bo)])
                            pc = ps[bo][:, 0:388].rearrange("p (s c) -> p s c", s=4)
                            self.branch_coef(pc[:, :, 64:65], rd, coef, gate, h, 0, bo)
                            P.op("dve", lambda e: e.tensor_tensor(
                                out=oacc[:, hl, :, :], in0=pc[:, :, 0:64], in1=coef[:, 0, :].unsqueeze(2).to_broadcast([128, 4, 64]), op=ALU.mult),
                                reads=["ns_coef"], writes=[("ps", bo), ("ns_oacc", hl)])
                            if Q >= 2:
                                dst = imp if hl == 0 else impt
                                dk_ = "ns_imp" if hl == 0 else "ns_impt"
                                P.op("dve", lambda e: e.tensor_tensor(
                                    out=dst[:, :, :], in0=pc[:, :, 65:97], in1=rd[:, 0, :].unsqueeze(2).to_broadcast([128, 4, 32]), op=ALU.mult),
                                    reads=["ns_rd"], writes=[("ps", bo), dk_])
                                if hl > 0:
                                    P.op("dve", lambda e: e.tensor_tensor(out=imp[:, :, :], in0=imp[:, :, :], in1=impt[:, :, :], op=ALU.add),
                                         reads=["ns_impt"], writes=["ns_imp"])
                        step()
                        defer(cmp_tail, 2)
                    if Q >= 2:
                        flush()
                    if Q >= 2:
                        for sub in range(4):
                            tile = Q * 4 + sub
                            c0 = 31 - 2 * tile
                            P.op("dve", lambda e, sub=sub, c0=c0: e.tensor_tensor(
                                out=imp[:, sub, :], in0=imp[:, sub, :], in1=self.consts[:, C_MUL + c0:C_MUL + c0 + 32], op=ALU.mult),
                                reads=["consts"], writes=["ns_imp"])
                            P.op("dve", lambda e, sub=sub, c0=c0: e.tensor_tensor(
                                out=imp[:, sub, :], in0=imp[:, sub, :], in1=self.consts[:, C_ADD + c0:C_ADD + c0 + 32], op=ALU.add),
                                reads=["consts"], writes=["ns_imp"])
                        P.op("dve", lambda e: e.memset(imp[:, :, 0:1], 3e9), writes=["ns_imp"])
                        for sub in range(4):
                            P.op("dve", lambda e, sub=sub: e.max(out=m8[:, sub, 0:8], in_=imp[:, sub, :]), reads=["ns_imp"], writes=["ns_m8"])
                            P.op("dve", lambda e, sub=sub: e.match_replace(out=impt[:, sub, :], in_to_replace=m8[:, sub, 0:8],
                                                                           in_values=imp[:, sub, :], imm_value=-1e30),
                                 reads=["ns_imp", "ns_m8"], writes=["ns_impt"])
                            P.op("dve", lambda e, sub=sub: e.max(out=m8[:, sub, 8:16], in_=impt[:, sub, :]), reads=["ns_impt"], writes=["ns_m8"])
                            P.op("dve", lambda e, sub=sub: e.tensor_scalar(
                                out=biasb[:, sub, 64:96], in0=imp[:, sub, :], scalar1=m8[:, sub, 15:16], scalar2=-16384.0,
                                op0=ALU.is_lt, op1=ALU.mult), reads=["ns_imp", "ns_m8"], writes=["ns_bias"])

                    def branch(hl, br, first, g=g, Q=Q):
                        h = g * 4 + hl
                        kT, kK, vv, vkey = ((kselT, 96, vsel, "vsel"), (kwinT, 64, vwin, "vwin"))[br]
                        bank_o = (B_SEL, B_WIN)[rr["o"]]
                        rr["o"] ^= 1
                        kc_lo = 0 if br == 0 else max(0, 4 * Q - 4)
                        kc_hi = 4 * Q + 3
                        for kc in range(kc_lo, kc_hi + 1):
                            d = kc - 4 * Q
                            lo = max(0, d)
                            hi = 3 if br == 0 else min(3, d + 4)
                            f0, f1 = lo * 128, (hi + 1) * 128
                            bS = next_s()
                            pt, ptk = next_pt()
                            kk = [("kselT", g, kc // 4), ("kselE", g)] if br == 0 else [("kwinT", g, kc // 4)]
                            qk = [("qaug", hl), ("qbias", hl)] if br == 0 else [("qaug", hl)]
                            mblk = None
                            if d >= 0:
                                mblk = (d, 0)
                            elif br == 1 and d + 4 <= 3:
                                mblk = (d + 4, 128)
                            P.op("pe", lambda e, bS=bS, kc=kc, f0=f0, f1=f1, mblk=mblk: e.matmul(
                                ps[bS][:, f0:f1], lhsT=kT[0:kK, g, kc * 128:(kc + 1) * 128], rhs=qaug[0:kK, hl, f0:f1],
                                start=True, stop=(mblk is None)), reads=kk + qk, writes=[("ps", bS)])
                            if mblk is not None:
                                P.op("pe", lambda e, bS=bS, mblk=mblk: e.matmul(
                                    ps[bS][:, mblk[0] * 128:(mblk[0] + 1) * 128], lhsT=ident, rhs=tri[:, mblk[1]:mblk[1] + 128],
                                    start=False, stop=True), reads=["ns_tri", "ns_ident"], writes=[("ps", bS)])
                            P.op("act", lambda e, bS=bS, pt=pt, f0=f0, f1=f1: e.activation(
                                out=pt[:, f0:f1], in_=ps[bS][:, f0:f1], func=AF.Exp, scale=0.125),
                                writes=[("ps", bS), ptk])

                            def tail(lo=lo, hi=hi, pt=pt, ptk=ptk, kc=kc, last=(kc == kc_hi)):
                                for sub in range(lo, hi + 1):
                                    P.op("pe", lambda e, sub=sub: e.matmul(
                                        ps[bank_o][:, sub * 65:(sub + 1) * 65], lhsT=pt[:, sub * 128:(sub + 1) * 128],
                                        rhs=vv[:, kc, g, :], start=(kc == kc_lo and sub == lo), stop=(kc == 4 * Q + sub), skip_group_check=True),
                                        reads=[ptk, (vkey, kc), vkey + "_1"], writes=[("ps", bank_o)])
                                if not last:
                                    return
                                po = ps[bank_o][:, 0:260].rearrange("p (s c) -> p s c", s=4)
                                self.branch_coef(po[:, :, 64:65], rd, coef, gate, h, br + 1, bank_o)
                                tmk = ("ns_tmpo", br)
                                P.op("dve", lambda e: e.tensor_tensor(
                                    out=tmpo[:, br, :, :], in0=po[:, :, 0:64], in1=coef[:, br + 1, :].unsqueeze(2).to_broadcast([128, 4, 64]), op=ALU.mult),
                                    reads=["ns_coef"], writes=[("ps", bank_o), tmk])
                                if first:
                                    P.op("dve", lambda e: e.tensor_tensor(out=oacc[:, hl, :, :], in0=oacc[:, hl, :, :], in1=tmpo[:, br, :, :], op=ALU.add),
                                         reads=[tmk], writes=[("ns_oacc", hl)])
                                    return
                                pi = (h // 2) % 2
                                P.op("dve", lambda e: e.tensor_tensor(
                                    out=opair[:, pi, :, (h % 2) * 64:(h % 2) * 64 + 64], in0=oacc[:, hl, :, :], in1=tmpo[:, br, :, :], op=ALU.add),
                                    reads=[tmk, ("ns_oacc", hl)], writes=[("ns_opair", pi, h % 2)])
                                if h % 2 == 1:
                                    def tr(c=h // 2, pi=pi):
                                        pb2 = ps[B_B][:, :].bitcast(BF16)
                                        for sub in range(4):
                                            P.op("pe", lambda e, sub=sub: e.transpose(
                                                out=pb2[:, sub * 128:(sub + 1) * 128], in_=opair[:, pi, sub, :], identity=ident),
                                                reads=[("ns_opair", pi, 0), ("ns_opair", pi, 1), "ns_ident"], writes=[("ps", B_B)])
                                        P.op("act", lambda e: e.activation(out=oT[:, c, :], in_=pb2[:, 0:512], func=AF.Copy),
                                             writes=[("ps", B_B), ("ns_oT", c)])
                                    defer(tr, 6)
                            step()
                            defer(tail, 2)

                    for hl in range(4):
                        branch(hl, 1, True)
                    if Q >= 2:
                        pb = ps[B_A][:, :].bitcast(BF16)
                        for sub in range(4):
                            P.op("pe", lambda e, sub=sub, pb=pb: e.transpose(out=pb[0:96, sub * 128:(sub + 1) * 128], in_=biasb[:, sub, :], identity=ident),
                                 reads=["ns_bias", "ns_ident"], writes=[("ps", B_A)])
                        for hl in range(4):
                            if hl % 2 == 0:
                                P.op("act", lambda e, hl=hl, pb=pb: e.activation(out=qaug[64:96, hl, :], in_=pb[64:96, 0:512], func=AF.Copy),
                                     writes=[("ps", B_A), ("qbias", hl)])
                            else:
                                P.op("dve", lambda e, hl=hl, pb=pb: e.tensor_copy(out=qaug[64:96, hl, :], in_=pb[64:96, 0:512]),
                                     writes=[("ps", B_A), ("qbias", hl)])
                    for hl in range(4):
                        branch(hl, 0, False)
                flush()
                slo = [self.load_w(self.d_wo[i], 4096) for i in range(2)]
                for mo in range(NKC):
                    sl, sk = slo[mo // 4]
                    b = B_A if mo % 2 == 0 else B_B
                    for kc in range(NKC):
                        o = ((mo % 4) * 8 + kc) * 128
                        P.op("pe", lambda e, b=b, o=o, kc=kc, sl=sl: e.matmul(
                            ps[b][:, :], lhsT=sl[:, o:o + 128], rhs=oT[:, kc, :], start=(kc == 0), stop=(kc == NKC - 1)),
                            reads=[sk, ("ns_oT", kc)], writes=[("ps", b)])
                    P.op("dve", lambda e, b=b, mo=mo, tq=tq: e.tensor_tensor(
                        out=self.hT[:, mo, tq], in0=ps[b][:, :], in1=self.hT[:, mo, tq], op=ALU.add),
                        writes=[("ps", b), ("hT", mo, Q)])

    def branch_coef(self, den, rd, coef, gate, h, br, bank):
        P = self.P
        rk = ("ns_rd", br)
        P.op("dve", lambda e: e.tensor_scalar(out=rd[:, br, :].unsqueeze(2), in0=den, scalar1=self.gcol(C_TINY), scalar2=None, op0=ALU.add),
             reads=["consts"], writes=[("ps", bank), rk, "ns_rd"])
        P.op("dve", lambda e: e.reciprocal(out=rd[:, br, :], in_=rd[:, br, :]), reads=[rk], writes=[rk, "ns_rd"])
        P.op("dve", lambda e: e.tensor_tensor(out=coef[:, br, :], in0=rd[:, br, :], in1=gate[:, :, 3 * h + br], op=ALU.mult),
             reads=[rk, "ns_gate"], writes=["ns_coef"])

    def build(self):
        nc, P = self.nc, self.P
        self.prologue()
        for ph in self.phases:
            if ph.startswith("ffn"):
                self.ffn(int(ph[3:]))
            elif ph == "conv":
                self.conv()
            elif ph == "kv":
                self.kv()
            elif ph == "nsa":
                self.nsa()
        self.epilogue()
        with nc.Block() as block:
            sems = {e: nc.alloc_semaphore(f"s_{e}") for e in Prog.ENGS}
            dsems = {q: [nc.alloc_semaphore(f"d{q}_{i}") for i in range(Prog.N_DMA_SEMS)] for q in ("pool", "sp")}
            P.emit(block, sems, dsems)
        return nc


def _prep_inputs(inputs, phases):
    f = lambda a: np.ascontiguousarray(np.asarray(a, np.float32))
    m = {}
    consts = np.zeros((128, NCONST), np.float32)
    consts[:, 0:32] = _fm(f(inputs["ffn_norm"]).reshape(4, D))
    consts[:, 32:48] = _fm(f(inputs["mix_norm"]))
    consts[:, 48:56] = _fm(f(inputs["kv_norm"]).reshape(1, D))
    consts[:, C_CONVW:C_CONVW + 24] = _fm(f(inputs["conv_w"]).reshape(3, D))
    kn = f(inputs["k_norm"])
    consts[0:64, C_DK:C_DK + 3] = kn.T
    consts[64:128, C_DK:C_DK + 3] = kn.T
    consts[0:64, C_DK + 3] = f(inputs["q_norm"])[0]
    consts[64:128, C_DK + 3] = f(inputs["q_norm"])[0]
    b1 = f(inputs["cmp_b1"])
    consts[:, C_B1:C_B1 + 4] = b1.reshape(2, 2, 128).transpose(2, 0, 1).reshape(128, 4)
    consts[:, C_EPS] = EPS
    m["consts"] = consts
    gu = f(inputs["ffn_w_gate_up"]).reshape(4, D, 2 * DFF)
    dn = f(inputs["ffn_w_down"]).reshape(4, DFF, D)
    m["w_gu"] = np.stack([_img_gu(gu[i]) for i in range(4)])
    m["w_dn"] = np.stack([_img_dn(dn[i]) for i in range(4)])
    m["w_cin"] = _img_cin(f(inputs["conv_w_in"])[0])
    m["w_cout"] = _img_sq(f(inputs["conv_w_out"])[0])
    kvw = f(inputs["kv_w"]).reshape(8, 128, 6, 4, 64)
    ra = kvw[:, :, [0, 1]]
    ra = ra.transpose(1, 3, 0, 2, 4).reshape(128, 4096)
    kb = kvw[:, :, [2, 4]]
    kb = kb.transpose(1, 2, 3, 0, 4).reshape(128, 4096)
    m["w_kvk"] = np.ascontiguousarray(np.stack([ra, kb]))
    vv = kvw[:, :, [3, 5]]
    m["w_kvv"] = np.ascontiguousarray(vv.transpose(1, 0, 2, 3, 4)).reshape(128, 4096)
    w1 = f(inputs["cmp_w1"]).reshape(2, 2, 16, 64, 256)
    m["w_c1"] = np.ascontiguousarray(w1.transpose(0, 1, 3, 2, 4)).reshape(2, 2, 64, 4096)
    w2 = f(inputs["cmp_w2"]).reshape(2, 2, 128, 64)
    m["w_c2"] = np.ascontiguousarray(w2.transpose(2, 0, 1, 3)).reshape(128, 256)
    pos = f(inputs["cmp_pos"])
    m["posT"] = np.ascontiguousarray(pos.transpose(0, 2, 1)).reshape(128, 32)
    wqg = f(inputs["nsa_w_qg"])[0]
    m["w_q"] = _img_sq(np.ascontiguousarray(wqg[:, :1024]))
    wg = wqg[:, 1024:].reshape(8, 128, 48)
    m["w_g"] = np.ascontiguousarray(wg.transpose(1, 0, 2)).reshape(128, 384)
    m["w_o"] = _img_sq(f(inputs["nsa_w_o"])[0])
    m.update(_mask_consts())
    m["consts"][:, C_MUL:C_MUL + 63] = m.pop("_mul")
    m["consts"][:, C_ADD:C_ADD + 63] = m.pop("_add")
    m["consts"][:, C_TINY] = 1e-30
    return m


_MASKS = {}


def _mask_consts():
    if _MASKS:
        return dict(_MASKS)
    key = np.arange(S)
    eind = (key[None, :] // 64 == np.arange(32)[:, None]).astype(np.float32)
    n = np.arange(128)
    valid = np.where((16 * n[:, None] + 31 <= key[None, :]) & (n[:, None] < 127), 0.0, -16384.0).astype(np.float32)
    p = np.arange(128)
    tri = np.concatenate([np.where(p[:, None] > p[None, :], -16384.0, 0.0), np.where(p[:, None] <= p[None, :], -16384.0, 0.0),
                          (p[:, None] == p[None, :]).astype(np.float64)], axis=1).astype(np.float32)
    j = np.arange(32)
    ovl = ((16 * n[:, None] < 64 * j[None, :] + 64) & (16 * n[:, None] + 32 > 64 * j[None, :]) & (n[:, None] < 127)).astype(np.float32)
    c = (p >= 64).astype(np.int64)[:, None]
    jr = (np.arange(63) - 31)[None, :]
    forced0 = jr == c
    forced1 = jr == c - 1
    future = jr > c
    mul = (~(forced0 | forced1 | future)).astype(np.float32)
    add = np.where(forced0, 2e9, np.where(forced1, 1e9, np.where(future, -1e30, 0.0))).astype(np.float32)
    _MASKS.update({"m_eind": eind, "m_valid": valid, "m_tri": tri, "m_ovl": ovl, "_mul": mul, "_add": add})
    return dict(_MASKS)


_NC_CACHE = {}
_RUN_KW = {}
_LAST = {}


def run_phases(inputs, phases, xT_list):
    key = tuple(phases)
    if key not in _NC_CACHE:
        _NC_CACHE[key] = Builder(phases).build()
    nc = _NC_CACHE[key]
    shared = _prep_inputs(inputs, phases)
    in_maps = []
    for xT in xT_list:
        d = dict(shared)
        d["xT"] = np.ascontiguousarray(xT, np.float32)
        in_maps.append(d)
    res = run_bass_kernel_spmd(nc, in_maps, core_ids=list(range(len(xT_list))), **_RUN_KW)
    _LAST["res"] = res
    return [r["yT"] for r in res.results]


def kernel(**inputs):
    x = np.asarray(inputs["x"], np.float32)
    xT = [np.ascontiguousarray(x[b].T) for b in range(x.shape[0])]
    yT = run_phases(inputs, ALL_PHASES, xT)
    return np.stack([np.ascontiguousarray(y.T) for y in yT]).astype(np.float32)
```

```python
from contextlib import ExitStack

import numpy as np
import concourse.bass as bass
import concourse.mybir as mybir
from concourse.bass_utils import run_bass_kernel_spmd

F32 = mybir.dt.float32
BF16 = mybir.dt.bfloat16
AF = mybir.ActivationFunctionType
ALU = mybir.AluOpType
AX = mybir.AxisListType


class Node:
    __slots__ = ("eng", "fn", "deps", "signal", "sem", "val", "dma", "seq")

    def __init__(self, eng, fn, dma):
        self.eng = eng
        self.fn = fn
        self.dma = dma
        self.deps = []
        self.signal = False
        self.sem = None
        self.val = 0
        self.seq = 0


class Prog:
    ENGS = ("pe", "act", "dve", "pool", "sp")
    N_DMA_SEMS = 16

    def __init__(self, nc):
        self.nc = nc
        self.nodes = {e: [] for e in self.ENGS}
        self.last_w = {}
        self.readers = {}
        self.dma_nodes = {"pool": [], "sp": [], "act": []}
        self.fence = []
        self.fenced = set()
        self.out_dmas = []

    def new_phase(self):
        self.fence = [self.nodes[e][-1] for e in self.ENGS if self.nodes[e]]
        self.fenced = set()

    def op(self, eng, fn, reads=(), writes=(), dma=False, arena=True, out=False):
        n = Node(eng, fn, dma)
        deps = []
        for k in reads:
            w = self.last_w.get(k)
            if w is not None:
                deps.append(w)
        for k in writes:
            w = self.last_w.get(k)
            if w is not None:
                deps.append(w)
            r = self.readers.get(k)
            if r:
                deps.extend(r.values())
        if arena and eng not in self.fenced:
            deps.extend(self.fence)
            self.fenced.add(eng)
        if dma:
            lst = self.dma_nodes[eng]
            i = len(lst)
            if i >= self.N_DMA_SEMS:
                deps.append(lst[i - self.N_DMA_SEMS])
            lst.append(n)
            if out:
                self.out_dmas.append(n)
        for k in reads:
            rk = id(n) if dma else eng
            self.readers.setdefault(k, {})[rk] = n
        for k in writes:
            self.last_w[k] = n
            self.readers[k] = {}
        seen = set()
        for d in deps:
            if d is n or id(d) in seen:
                continue
            seen.add(id(d))
            if d.eng == "pe" and eng == "pe" and not d.dma and not dma:
                continue
            n.deps.append(d)
            d.signal = True
        n.seq = len(self.nodes[eng])
        self.nodes[eng].append(n)
        return n

    def emit(self, block, sems, dma_sems):
        nc = self.nc
        for e in self.ENGS:
            c = 0
            for n in self.nodes[e]:
                if n.dma:
                    continue
                if n.signal:
                    c += 1
                    n.sem = sems[e]
                    n.val = c
        for q, lst in self.dma_nodes.items():
            if not lst:
                continue
            qs = dma_sems[q]
            cnt = [0] * len(qs)
            for i, n in enumerate(lst):
                s = i % len(qs)
                cnt[s] += 16
                n.sem = qs[s]
                n.val = cnt[s]
                n.signal = True
        out_dmas = self.out_dmas

        def run(e, engine):
            waited = {}
            for n in self.nodes[e]:
                need = {}
                for d in n.deps:
                    k = d.sem.num
                    if need.get(k, (0, None))[0] < d.val:
                        need[k] = (d.val, d.sem)
                for k, (v, s) in need.items():
                    if waited.get(k, 0) < v:
                        engine.wait_ge(s, v)
                        waited[k] = v
                ins = n.fn(engine)
                if n.signal:
                    ins.then_inc(n.sem, 16 if n.dma else 1)
            if e == "sp":
                for d in out_dmas:
                    if waited.get(d.sem.num, 0) < d.val:
                        engine.wait_ge(d.sem, d.val)
                        waited[d.sem.num] = d.val

        block.tensor(lambda eng: run("pe", eng))
        block.scalar(lambda eng: run("act", eng))
        block.vector(lambda eng: run("dve", eng))
        block.gpsimd(lambda eng: run("pool", eng))
        block.sync(lambda eng: run("sp", eng))


D = 1024
S = 2048
DFF = 2816
NKC = 8
NJ = 22
NS = 5
SLOT = 4096
EPS = 1e-6
NCONST = 224
C_GAIN = 0
C_CONVW = 56
C_DK = 80
C_B1 = 84
C_EPS = 88
C_TINY = 89
C_MUL = 96
C_ADD = 160
KSLOTS = (0, 1, 2, 4)
GELU_C = 1.5957691216057308
ALL_PHASES = ("ffn0", "conv", "ffn1", "kv", "ffn2", "nsa", "ffn3")


def _img_gu(w):
    a = w.reshape(8, 128, 2, 11, 2, 128)
    return np.ascontiguousarray(a.transpose(3, 1, 4, 2, 0, 5)).reshape(11, 128, 4096)


def _img_dn(w):
    a = w.reshape(2, 11, 128, 4, 2, 128)
    return np.ascontiguousarray(a.transpose(0, 3, 2, 4, 1, 5)).reshape(2, 4, 128, 2816)


def _img_cin(w):
    a = w.reshape(8, 128, 3, 8, 128)
    return np.ascontiguousarray(a.transpose(3, 1, 2, 0, 4)).reshape(8, 128, 3072)


def _img_sq(w):
    a = w.reshape(8, 128, 2, 4, 128)
    return np.ascontiguousarray(a.transpose(2, 1, 3, 0, 4)).reshape(2, 128, 4096)


def _fm(v):
    v = np.asarray(v, np.float32).reshape(-1, 8, 128)
    return np.ascontiguousarray(v.transpose(2, 0, 1)).reshape(128, -1)


class Builder:
    def __init__(self, phases, debug=False):
        self.phases = tuple(phases)
        self.debug = debug
        nc = self.nc = bass.Bass("TRN2", target_bir_lowering=False)
        self.P = Prog(nc)
        self.bank_rr = 0
        self.slot_rr = 0
        self.uid = 0
        dt = nc.dram_tensor
        self.d_x = dt("xT", [D, S], F32, kind="ExternalInput").ap()
        self.d_y = dt("yT", [D, S], F32, kind="ExternalOutput").ap()
        self.d_consts = dt("consts", [128, NCONST], F32, kind="ExternalInput").ap()
        self.d_wgu = dt("w_gu", [4, 11, 128, 4096], F32, kind="ExternalInput").ap()
        self.d_wdn = dt("w_dn", [4, 2, 4, 128, 2816], F32, kind="ExternalInput").ap()
        self.d_cin = dt("w_cin", [8, 128, 3072], F32, kind="ExternalInput").ap()
        self.d_cout = dt("w_cout", [2, 128, 4096], F32, kind="ExternalInput").ap()
        self.d_kvk = dt("w_kvk", [2, 128, 4096], F32, kind="ExternalInput").ap()
        self.d_kvv = dt("w_kvv", [128, 4096], F32, kind="ExternalInput").ap()
        self.d_w1 = dt("w_c1", [2, 2, 64, 4096], F32, kind="ExternalInput").ap()
        self.d_w2 = dt("w_c2", [128, 256], F32, kind="ExternalInput").ap()
        self.d_pos = dt("posT", [128, 32], F32, kind="ExternalInput").ap()
        self.d_eind = dt("m_eind", [32, S], F32, kind="ExternalInput").ap()
        self.d_valid = dt("m_valid", [128, S], F32, kind="ExternalInput").ap()
        self.d_tri = dt("m_tri", [128, 384], F32, kind="ExternalInput").ap()
        self.d_ovl = dt("m_ovl", [128, 32], F32, kind="ExternalInput").ap()
        self.d_wq = dt("w_q", [2, 128, 4096], F32, kind="ExternalInput").ap()
        self.d_wg = dt("w_g", [128, 384], F32, kind="ExternalInput").ap()
        self.d_wo = dt("w_o", [2, 128, 4096], F32, kind="ExternalInput").ap()
        A = nc.alloc_sbuf_tensor
        self.hT = A("hT", [128, NKC, S], F32)
        self.slots = [A(f"wslot{i}", [128, SLOT], BF16) for i in range(NS)]
        self.consts = A("consts_sb", [128, NCONST], F32)
        self.ones = A("ones_bf", [128, 128], BF16)
        self.onesbd = A("onesbd_bf", [128, 128], BF16)
        self.ps = [nc.alloc_psum_tensor(f"psb{i}", [128, 512], F32) for i in range(8)]

    def key(self, name):
        self.uid += 1
        return (name, self.uid)

    def bank(self):
        b = self.bank_rr
        self.bank_rr = (b + 1) % 8
        return b

    def load_w(self, src_ap, nelem, parts=128, p0=0):
        i = self.slot_rr
        self.slot_rr = (i + 1) % NS
        sl = self.slots[i]
        self.P.op("pool", lambda e: e.dma_start(out=sl[p0:p0 + parts, 0:nelem], in_=src_ap),
                  writes=[("slot", i)], dma=True, arena=False)
        return sl, ("slot", i)

    def gcol(self, c):
        return self.consts[:, c:c + 1]

    def prologue(self):
        P = self.P
        P.op("sp", lambda e: e.dma_start(out=self.consts[:, :], in_=self.d_consts), writes=["consts"], dma=True, arena=False)
        P.op("dve", lambda e: e.memset(self.ones[:, :], 1.0), writes=["ones"], arena=False)
        P.op("dve", lambda e: e.memset(self.onesbd[:, :], 0.0), writes=["onesbd"], arena=False)
        P.op("dve", lambda e: e.memset(self.onesbd[0:64, 0:64], 1.0), writes=["onesbd"], arena=False)
        P.op("dve", lambda e: e.memset(self.onesbd[64:128, 64:128], 1.0), writes=["onesbd"], arena=False)
        xv = self.d_x.rearrange("(c p) t -> p c t", p=128)
        for c in range(NKC):
            P.op("sp", lambda e, c=c: e.dma_start(out=self.hT[:, c, :], in_=xv[:, c, :]),
                 writes=[("hT", c, q) for q in range(4)], dma=True, arena=False)

    def epilogue(self):
        P = self.P
        yv = self.d_y.rearrange("(c p) t -> p c t", p=128)
        for c in range(NKC):
            P.op("sp", lambda e, c=c: e.dma_start(out=yv[:, c, :], in_=self.hT[:, c, :]),
                 reads=[("hT", c, q) for q in range(4)], dma=True, arena=False, out=True)

    def rmsnorm_tile(self, q, gain_col, xn, xn_off, xn_key, work, bank=None, lnexp=True):
        P = self.P
        sq, rstd = work
        tok = slice(q * 512, (q + 1) * 512)
        b = self.bank() if bank is None else bank
        for c in range(NKC):
            s = sq[c % 2]
            sk = ("sq", c % 2)
            P.op("act", lambda e, c=c, s=s: e.activation(out=s[:, :], in_=self.hT[:, c, tok], func=AF.Square),
                 reads=[("hT", c, q)], writes=[sk])
            P.op("pe", lambda e, c=c, s=s: e.matmul(self.ps[b][:, :], lhsT=self.ones[:, :], rhs=s[:, :],
                                                   start=(c == 0), stop=(c == NKC - 1)),
                 reads=[sk, "ones"], writes=[("ps", b)])
        if lnexp:
            P.op("act", lambda e: e.activation(out=rstd[:, :], in_=self.ps[b][:, :], func=AF.Ln,
                                               bias=self.gcol(C_EPS), scale=1.0 / D),
                 reads=["consts"], writes=[("ps", b), "rstd"])
            P.op("act", lambda e: e.activation(out=rstd[:, :], in_=rstd[:, :], func=AF.Exp, scale=-0.5),
                 reads=["rstd"], writes=["rstd"])
        else:
            P.op("act", lambda e: e.activation(out=rstd[:, :], in_=self.ps[b][:, :], func=AF.Sqrt,
                                               bias=self.gcol(C_EPS), scale=1.0 / D),
                 reads=["consts"], writes=[("ps", b), "rstd"])
            P.op("dve", lambda e: e.reciprocal(out=rstd[:, :], in_=rstd[:, :]), reads=["rstd"], writes=["rstd"])
        for c in range(NKC):
            P.op("dve", lambda e, c=c: e.scalar_tensor_tensor(
                out=xn[:, c, xn_off:xn_off + 512], in0=self.hT[:, c, tok], scalar=self.gcol(gain_col + c),
                in1=rstd[:, :], op0=ALU.mult, op1=ALU.mult),
                reads=[("hT", c, q), "rstd", "consts"], writes=[(xn_key, c, xn_off // 512)])

    def ffn(self, li):
        nc, P = self.nc, self.P
        P.new_phase()
        with (
            nc.sbuf_tensor(f"ffn_xn{li}", [128, NKC, 1024], BF16) as xn,
            nc.sbuf_tensor(f"ffn_act{li}", [128, 11, 1024], BF16) as act,
            nc.sbuf_tensor(f"ffn_sq0{li}", [128, 512], BF16) as sq0,
            nc.sbuf_tensor(f"ffn_sq1{li}", [128, 512], BF16) as sq1,
            nc.sbuf_tensor(f"ffn_rstd{li}", [128, 512], F32) as rstd,
            nc.sbuf_tensor(f"ffn_sa0{li}", [128, 512], F32) as sa0,
            nc.sbuf_tensor(f"ffn_sa1{li}", [128, 512], F32) as sa1,
            nc.sbuf_tensor(f"ffn_sa2{li}", [128, 512], F32) as sa2,
        ):
            sas = [sa0, sa1, sa2]
            sa_rr = 0
            for half in range(2):
                for tt in range(2):
                    self.rmsnorm_tile(half * 2 + tt, C_GAIN + li * 8, xn, tt * 512, "ffn_xn", ((sq0, sq1), rstd))
                for jh in range(2):
                    jlist = list(range(jh * 11, jh * 11 + 11))
                    loaded = {}
                    for j in jlist:
                        jp, jj = divmod(j, 2)
                        if jp not in loaded:
                            loaded[jp] = self.load_w(self.d_wgu[li, jp], 4096)
                        sl, sk = loaded[jp]
                        jl = j - jh * 11
                        for tt in range(2):
                            ba, bb = self.bank(), self.bank()
                            tk = slice(tt * 512, (tt + 1) * 512)
                            for ab, b in ((0, ba), (1, bb)):
                                for kc in range(NKC):
                                    o = ((jj * 2 + ab) * 8 + kc) * 128
                                    P.op("pe", lambda e, b=b, o=o, kc=kc, sl=sl, tk=tk: e.matmul(
                                        self.ps[b][:, :], lhsT=sl[:, o:o + 128], rhs=xn[:, kc, tk],
                                        start=(kc == 0), stop=(kc == NKC - 1)),
                                        reads=[sk, ("ffn_xn", kc, tt)], writes=[("ps", b)])
                            sa = sas[sa_rr]
                            sak = ("ffn_sa", sa_rr)
                            sa_rr = (sa_rr + 1) % 3
                            P.op("act", lambda e, sa=sa, ba=ba: e.activation(out=sa[:, :], in_=self.ps[ba][:, :], func=AF.Silu),
                                 writes=[("ps", ba), sak])
                            P.op("dve", lambda e, sa=sa, bb=bb, jl=jl, tk=tk: e.tensor_tensor(
                                out=act[:, jl, tk], in0=sa[:, :], in1=self.ps[bb][:, :], op=ALU.mult),
                                reads=[sak], writes=[("ps", bb), ("ffn_act", jl, tt)])
                    for mp in range(4):
                        sl, sk = self.load_w(self.d_wdn[li, jh, mp], 2816)
                        for ml in range(2):
                            m = mp * 2 + ml
                            for tt in range(2):
                                b = self.bank()
                                q = half * 2 + tt
                                tk = slice(tt * 512, (tt + 1) * 512)
                                for jl in range(11):
                                    o = (ml * 11 + jl) * 128
                                    P.op("pe", lambda e, b=b, o=o, jl=jl, sl=sl, tk=tk: e.matmul(
                                        self.ps[b][:, :], lhsT=sl[:, o:o + 128], rhs=act[:, jl, tk],
                                        start=(jl == 0), stop=(jl == 10)),
                                        reads=[sk, ("ffn_act", jl, tt)], writes=[("ps", b)])
                                P.op("dve", lambda e, b=b, m=m, q=q: e.scalar_tensor_tensor(
                                    out=self.hT[:, m, q * 512:(q + 1) * 512], in0=self.ps[b][:, :], scalar=0.5,
                                    in1=self.hT[:, m, q * 512:(q + 1) * 512], op0=ALU.mult, op1=ALU.add),
                                    reads=[], writes=[("ps", b), ("hT", m, q)])

    def conv(self):
        nc, P = self.nc, self.P
        P.new_phase()
        with (
            nc.sbuf_tensor("cv_xn", [128, NKC, S], BF16) as xn,
            nc.sbuf_tensor("cv_gated", [128, NKC, S], BF16) as gated,
            nc.sbuf_tensor("cv_sq0", [128, 512], BF16) as sq0,
            nc.sbuf_tensor("cv_sq1", [128, 512], BF16) as sq1,
            nc.sbuf_tensor("cv_rstd", [128, 512], F32) as rstd,
            nc.sbuf_tensor("cv_csb", [128, 512], F32) as csb,
            nc.sbuf_tensor("cv_v", [128, 2, 514], F32) as vbuf,
            nc.sbuf_tensor("cv_acc", [128, 2, 512], F32) as accb,
        ):
            for q in range(4):
                self.rmsnorm_tile(q, C_GAIN + 32, xn, q * 512, "cv_xn", ((sq0, sq1), rstd))
            vi = 0
            for m in range(NKC):
                sl, sk = self.load_w(self.d_cin[m], 3072)
                for q in range(4):
                    tk = slice(q * 512, (q + 1) * 512)
                    bs = [self.bank(), self.bank(), self.bank()]
                    for s3 in range(3):
                        for kc in range(NKC):
                            o = (s3 * 8 + kc) * 128
                            P.op("pe", lambda e, b=bs[s3], o=o, kc=kc, sl=sl, tk=tk: e.matmul(
                                self.ps[b][:, :], lhsT=sl[:, o:o + 128], rhs=xn[:, kc, tk],
                                start=(kc == 0), stop=(kc == NKC - 1)),
                                reads=[sk, ("cv_xn", kc, q)], writes=[("ps", bs[s3])])
                    v = vbuf[:, vi, :]
                    vk = ("cv_v", vi)
                    vprev = vbuf[:, 1 - vi, :]
                    vpk = ("cv_v", 1 - vi)
                    acc = accb[:, vi, :]
                    ak = ("cv_acc", vi)
                    vi = 1 - vi
                    if q == 0:
                        P.op("dve", lambda e, v=v: e.memset(v[:, 0:2], 0.0), writes=[vk])
                    else:
                        P.op("dve", lambda e, v=v, vprev=vprev: e.tensor_copy(out=v[:, 0:2], in_=vprev[:, 512:514]),
                             reads=[vpk], writes=[vk])
                    P.op("act", lambda e, b=bs[1]: e.activation(out=csb[:, :], in_=self.ps[b][:, :], func=AF.Copy),
                         writes=[("ps", bs[1]), "cv_csb"])
                    P.op("dve", lambda e, v=v, b=bs[2]: e.tensor_tensor(out=v[:, 2:514], in0=csb[:, :], in1=self.ps[b][:, :], op=ALU.mult),
                         reads=["cv_csb"], writes=[("ps", bs[2]), vk])
                    P.op("dve", lambda e, v=v, acc=acc, m=m: e.tensor_scalar(
                        out=acc, in0=v[:, 0:512], scalar1=self.gcol(C_CONVW + 0 * 8 + m), scalar2=None, op0=ALU.mult),
                        reads=[vk, "consts"], writes=[ak])
                    for k in (1, 2):
                        P.op("dve", lambda e, v=v, acc=acc, m=m, k=k: e.scalar_tensor_tensor(
                            out=acc, in0=v[:, k:k + 512], scalar=self.gcol(C_CONVW + k * 8 + m), in1=acc,
                            op0=ALU.mult, op1=ALU.add),
                            reads=[vk, ak, "consts"], writes=[ak])
                    P.op("dve", lambda e, acc=acc, b=bs[0], m=m, tk=tk: e.tensor_tensor(
                        out=gated[:, m, tk], in0=acc, in1=self.ps[b][:, :], op=ALU.mult),
                        reads=[ak], writes=[("ps", bs[0]), ("cv_gated", m, q)])
            for mh in range(2):
                sl, sk = self.load_w(self.d_cout[mh], 4096)
                for ml in range(4):
                    mo = mh * 4 + ml
                    for q in range(4):
                        tk = slice(q * 512, (q + 1) * 512)
                        b = self.bank()
                        for kc in range(NKC):
                            o = (ml * 8 + kc) * 128
                            P.op("pe", lambda e, b=b, o=o, kc=kc, sl=sl, tk=tk: e.matmul(
                                self.ps[b][:, :], lhsT=sl[:, o:o + 128], rhs=gated[:, kc, tk],
                                start=(kc == 0), stop=(kc == NKC - 1)),
                                reads=[sk, ("cv_gated", kc, q)], writes=[("ps", b)])
                        P.op("dve", lambda e, b=b, mo=mo, tk=tk: e.tensor_tensor(
                            out=self.hT[:, mo, tk], in0=self.ps[b][:, :], in1=self.hT[:, mo, tk], op=ALU.add),
                            writes=[("ps", b), ("hT", mo, q)])

    def dknorm(self, b, n, gain_col, dest, dest_keys, work, b2):
        P = self.P
        sqd, rs = work
        src = self.ps[b][0:64, 0:n]
        P.op("act", lambda e: e.activation(out=sqd[0:64, 0:n], in_=src, func=AF.Square),
             writes=[("ps", b), "dk_sq"])
        P.op("pe", lambda e: e.matmul(self.ps[b2][0:64, 0:n], lhsT=self.ones[0:64, 0:64], rhs=sqd[0:64, 0:n],
                                      start=True, stop=True),
             reads=["dk_sq", "ones"], writes=[("ps", b2)])
        P.op("act", lambda e: e.activation(out=rs[0:64, 0:n], in_=self.ps[b2][0:64, 0:n], func=AF.Ln,
                                           bias=self.consts[0:64, C_EPS:C_EPS + 1], scale=1.0 / 64),
             reads=["consts"], writes=[("ps", b2), "dk_rs"])
        P.op("act", lambda e: e.activation(out=rs[0:64, 0:n], in_=rs[0:64, 0:n], func=AF.Exp, scale=-0.5),
             reads=["dk_rs"], writes=["dk_rs"])
        rsv = rs[0:64, 0:n]
        srcv = src
        if len(dest.shape) == 3:
            g_ = dest.shape[1]
            rsv = rsv.rearrange("p (g n) -> p g n", g=g_)
            srcv = srcv.rearrange("p (g n) -> p g n", g=g_)
        P.op("dve", lambda e: e.scalar_tensor_tensor(out=dest, in0=srcv, scalar=self.consts[0:64, gain_col:gain_col + 1],
                                                     in1=rsv, op0=ALU.mult, op1=ALU.mult),
             reads=["dk_rs", "consts"], writes=[("ps", b)] + list(dest_keys))

    def kv(self):
        nc, P = self.nc, self.P
        P.new_phase()
        A = nc.alloc_sbuf_tensor
        self.kselT = A("kselT", [96, 4, S], BF16)
        self.kwinT = A("kwinT", [64, 4, S], BF16)
        self.vsel = A("vsel", [128, 16, 4, 65], BF16)
        self.vwin = A("vwin", [128, 16, 4, 65], BF16)
        self.kcmpT = A("kcmpT", [64, 4, 128], BF16)
        self.vcmp = A("vcmp", [128, 4, 97], BF16)
        kselT, kwinT, vsel, vwin, kcmpT, vcmp = self.kselT, self.kwinT, self.vsel, self.vwin, self.kcmpT, self.vcmp
        with (
            nc.sbuf_tensor("kv_xn", [128, NKC, 512], BF16) as xn,
            nc.sbuf_tensor("kv_raw", [128, 4, 16, 128], BF16) as rawT,
            nc.sbuf_tensor("kv_sq0", [128, 512], BF16) as sq0,
            nc.sbuf_tensor("kv_sq1", [128, 512], BF16) as sq1,
            nc.sbuf_tensor("kv_rstd", [128, 512], F32) as rstd,
            nc.sbuf_tensor("kv_sqd", [128, 512], BF16) as sqd,
            nc.sbuf_tensor("kv_rs", [128, 512], F32) as rs,
            nc.sbuf_tensor("kv_pos", [128, 32], BF16) as posT,
            nc.sbuf_tensor("kv_w2", [128, 256], BF16) as w2t,
            nc.sbuf_tensor("kv_u", [128, 508], F32) as ubuf,
            nc.sbuf_tensor("kv_t", [128, 508], F32) as tbuf,
            nc.sbuf_tensor("kv_cb", [128, 1], F32) as cb,
            nc.sbuf_tensor("kv_h1", [128, 2, 508], BF16) as h1,
        ):
            for g in range(4):
                P.op("pool", lambda e, g=g: e.dma_start(out=kselT[64:96, g, :], in_=self.d_eind), writes=[("kselE", g)], dma=True)
                P.op("pool", lambda e, g=g: e.dma_start(out=vcmp[:, g, 65:97], in_=self.d_ovl), writes=[("vcmp_o", g)], dma=True)
            P.op("pool", lambda e: e.dma_start(out=posT[:, :], in_=self.d_pos), writes=["kv_pos"], dma=True)
            P.op("pool", lambda e: e.dma_start(out=w2t[:, :], in_=self.d_w2), writes=["kv_w2"], dma=True)
            P.op("dve", lambda e: e.memset(kcmpT[:, :, :], 0.0), writes=["kcmpT"])
            P.op("dve", lambda e: e.memset(vcmp[:, :, 0:64], 0.0), writes=[("vcmp_v", g) for g in range(4)])
            P.op("dve", lambda e: e.memset(vcmp[:, :, 64:65], 1.0), writes=["vcmp_1"])
            P.op("dve", lambda e: e.memset(vsel[:, :, :, 64:65], 1.0), writes=["vsel_1"])
            P.op("dve", lambda e: e.memset(vwin[:, :, :, 64:65], 1.0), writes=["vwin_1"])
            slk = [self.load_w(self.d_kvk[i], 4096) for i in range(2)]
            slv, slvk = self.load_w(self.d_kvv, 4096)
            for q in range(4):
                tk = slice(q * 512, (q + 1) * 512)
                self.rmsnorm_tile(q, C_GAIN + 48, xn, 0, "kv_xn", ((sq0, sq1), rstd))
                slA, skA = slk[0]
                for g in range(4):
                    b = self.bank()
                    for kc in range(NKC):
                        o = (g * 8 + kc) * 128
                        P.op("pe", lambda e, b=b, o=o, kc=kc: e.matmul(
                            self.ps[b][:, :], lhsT=slA[:, o:o + 128], rhs=xn[:, kc, :],
                            start=(kc == 0), stop=(kc == NKC - 1)),
                            reads=[skA, ("kv_xn", kc, 0)], writes=[("ps", b)])
                    P.op("act", lambda e, b=b, g=g, q=q: e.activation(
                        out=rawT[:, g, :, q * 32:(q + 1) * 32], in_=self.ps[b][:, :].rearrange("p (n l) -> p l n", l=16), func=AF.Copy),
                        writes=[("ps", b), ("kv_raw", g, q)])
                slB, skB = slk[1]
                for g in range(4):
                    b = self.bank()
                    b2 = self.bank()
                    for kc in range(NKC):
                        o = (g * 8 + kc) * 128
                        P.op("pe", lambda e, b=b, o=o, kc=kc: e.matmul(
                            self.ps[b][:, :], lhsT=slB[:, o:o + 128], rhs=xn[:, kc, :],
                            start=(kc == 0), stop=(kc == NKC - 1)),
                            reads=[skB, ("kv_xn", kc, 0)], writes=[("ps", b)])
                    P.op("act", lambda e, b=b: e.activation(out=sqd[:, :], in_=self.ps[b][:, :], func=AF.Square),
                         writes=[("ps", b), "dk_sq"])
                    P.op("pe", lambda e, b2=b2: e.matmul(self.ps[b2][:, :], lhsT=self.onesbd[:, :], rhs=sqd[:, :], start=True, stop=True),
                         reads=["dk_sq", "onesbd"], writes=[("ps", b2)])
                    P.op("act", lambda e, b2=b2: e.activation(out=rs[:, :], in_=self.ps[b2][:, :], func=AF.Ln,
                                                              bias=self.gcol(C_EPS), scale=1.0 / 64),
                         reads=["consts"], writes=[("ps", b2), "dk_rs"])
                    P.op("act", lambda e: e.activation(out=rs[:, :], in_=rs[:, :], func=AF.Exp, scale=-0.5),
                         reads=["dk_rs"], writes=["dk_rs"])
                    P.op("dve", lambda e, b=b, g=g, tk=tk: e.scalar_tensor_tensor(
                        out=kselT[0:64, g, tk], in0=self.ps[b][0:64, :], scalar=self.consts[0:64, C_DK + 1:C_DK + 2],
                        in1=rs[0:64, :], op0=ALU.mult, op1=ALU.mult),
                        reads=["dk_rs", "consts"], writes=[("ps", b), ("kselT", g, q)])
                    P.op("dve", lambda e, b=b, g=g, tk=tk: e.scalar_tensor_tensor(
                        out=kwinT[0:64, g, tk], in0=self.ps[b][64:128, :], scalar=self.consts[64:128, C_DK + 2:C_DK + 3],
                        in1=rs[64:128, :], op0=ALU.mult, op1=ALU.mult),
                        reads=["dk_rs", "consts"], writes=[("ps", b), ("kwinT", g, q)])
                for sub in range(4):
                    tt = q * 4 + sub
                    b = self.bank()
                    for kc in range(NKC):
                        P.op("pe", lambda e, b=b, kc=kc, sub=sub: e.matmul(
                            self.ps[b][:, :], lhsT=xn[:, kc, sub * 128:(sub + 1) * 128], rhs=slv[:, kc * 512:(kc + 1) * 512],
                            start=(kc == 0), stop=(kc == NKC - 1)),
                            reads=[slvk, ("kv_xn", kc, 0)], writes=[("ps", b)])
                    P.op("act", lambda e, b=b, tt=tt: e.activation(
                        out=vsel[:, tt, :, 0:64], in_=self.ps[b][:, 0:256].rearrange("p (g d) -> p g d", g=4), func=AF.Copy),
                        writes=[("ps", b), ("vsel", tt)])
                    P.op("dve", lambda e, b=b, tt=tt: e.tensor_copy(
                        out=vwin[:, tt, :, 0:64], in_=self.ps[b][:, 256:512].rearrange("p (g d) -> p g d", g=4)),
                        writes=[("ps", b), ("vwin", tt)])
            raw_keys = lambda s_: [("kv_raw", g, q) for g in range(4) for q in range(4)]
            for s_ in range(2):
                pl = slice(s_ * 64, s_ * 64 + 64)
                w1s = [self.load_w(self.d_w1[s_, hf], 4096, parts=64, p0=s_ * 64) for hf in range(2)]
                for hc in range(2):
                    b = self.bank()
                    b1 = self.bank()
                    for l in range(32):
                        sl, sk = w1s[l // 16]
                        o = (l % 16) * 256 + hc * 128
                        P.op("pe", lambda e, b=b, sl=sl, o=o, l=l, pl=pl: e.matmul(
                            self.ps[b][:, 0:508], lhsT=sl[pl, o:o + 128],
                            rhs=rawT[pl, :, l % 16, (l // 16):(l // 16) + 127], start=(l == 0), stop=(l == 31)),
                            reads=[sk] + raw_keys(s_), writes=[("ps", b)])
                    for l in range(32):
                        sl, sk = w1s[l // 16]
                        o = (l % 16) * 256 + hc * 128
                        P.op("pe", lambda e, b1=b1, sl=sl, o=o, l=l, pl=pl: e.matmul(
                            self.ps[b1][:, 0:1], lhsT=sl[pl, o:o + 128],
                            rhs=posT[pl, l:l + 1], start=(l == 0), stop=(l == 31)),
                            reads=[sk, "kv_pos"], writes=[("ps", b1)])
                    P.op("dve", lambda e, b1=b1, s_=s_, hc=hc: e.tensor_scalar(
                        out=cb[:, :], in0=self.ps[b1][:, 0:1], scalar1=self.gcol(C_B1 + s_ * 2 + hc), scalar2=None, op0=ALU.add),
                        reads=["consts"], writes=[("ps", b1), "kv_cb"])
                    P.op("dve", lambda e, b=b: e.tensor_scalar(
                        out=ubuf[:, :], in0=self.ps[b][:, 0:508], scalar1=cb[:, 0:1], scalar2=None, op0=ALU.add),
                        reads=["kv_cb"], writes=[("ps", b), "kv_u"])
                    P.op("dve", lambda e: e.tensor_tensor(out=tbuf[:, :], in0=ubuf[:, :], in1=ubuf[:, :], op=ALU.mult),
                         reads=["kv_u"], writes=["kv_t"])
                    P.op("dve", lambda e: e.tensor_scalar(out=tbuf[:, :], in0=tbuf[:, :], scalar1=0.044715, scalar2=1.0,
                                                          op0=ALU.mult, op1=ALU.add), reads=["kv_t"], writes=["kv_t"])
                    P.op("dve", lambda e: e.tensor_tensor(out=tbuf[:, :], in0=tbuf[:, :], in1=ubuf[:, :], op=ALU.mult),
                         reads=["kv_t", "kv_u"], writes=["kv_t"])
                    P.op("act", lambda e: e.activation(out=tbuf[:, :], in_=tbuf[:, :], func=AF.Sigmoid, scale=GELU_C),
                         reads=["kv_t"], writes=["kv_t"])
                    P.op("dve", lambda e, hc=hc: e.tensor_tensor(out=h1[:, hc, :], in0=tbuf[:, :], in1=ubuf[:, :], op=ALU.mult),
                         reads=["kv_t", "kv_u"], writes=[("kv_h1", hc)])
                if s_ == 0:
                    b = self.bank()
                    for hc in range(2):
                        P.op("pe", lambda e, b=b, hc=hc: e.matmul(
                            self.ps[b][0:64, 0:508], lhsT=w2t[:, hc * 64:(hc + 1) * 64], rhs=h1[:, hc, :],
                            start=(hc == 0), stop=(hc == 1)),
                            reads=["kv_w2", ("kv_h1", hc)], writes=[("ps", b)])
                    self.dknorm(b, 508, C_DK + 0, kcmpT[0:64, :, 0:127], ["kcmpT"], (sqd, rs), self.bank())
                else:
                    for g in range(4):
                        b = self.bank()
                        for hc in range(2):
                            P.op("pe", lambda e, b=b, hc=hc, g=g: e.matmul(
                                self.ps[b][0:127, 0:64], lhsT=h1[:, hc, g * 127:(g + 1) * 127],
                                rhs=w2t[:, (2 + hc) * 64:(3 + hc) * 64], start=(hc == 0), stop=(hc == 1)),
                                reads=["kv_w2", ("kv_h1", hc)], writes=[("ps", b)])
                        P.op("act", lambda e, b=b, g=g: e.activation(out=vcmp[0:127, g, 0:64], in_=self.ps[b][0:127, 0:64], func=AF.Copy),
                             writes=[("ps", b), ("vcmp_v", g)])
            if self.debug:
                for nm, t in (("kselT", kselT), ("kwinT", kwinT), ("vsel", vsel), ("vwin", vwin), ("kcmpT", kcmpT), ("vcmp", vcmp)):
                    shp = list(t.shape)
                    dd = nc.dram_tensor("dbg_" + nm, shp, BF16, kind="ExternalOutput").ap()
                    idx = tuple(slice(None) for _ in shp)
                    P.op("sp", lambda e, dd=dd, t=t, idx=idx: e.dma_start(out=dd[idx], in_=t[idx]),
                         reads=[k for k in list(P.last_w.keys()) if isinstance(k, (tuple, str))], dma=True, out=True)

    def nsa(self):
        nc, P = self.nc, self.P
        P.new_phase()
        kselT, kwinT, vsel, vwin, kcmpT, vcmp = self.kselT, self.kwinT, self.vsel, self.vwin, self.kcmpT, self.vcmp
        ps = self.ps
        B_S = (0, 1, 2)
        B_CMP, B_SEL, B_WIN, B_A, B_B = 3, 4, 5, 6, 7
        NPT = 6
        with ExitStack() as st:
            T = lambda name, shape, dt_: st.enter_context(nc.sbuf_tensor(name, shape, dt_))
            xn = T("ns_xn", [128, NKC, 512], BF16)
            sqd2 = T("ns_sqd2", [128, 2, 512], BF16)
            rs2 = T("ns_rs2", [128, 2, 512], F32)
            validT = T("ns_valid", [128, S], BF16)
            tri = T("ns_tri", [128, 384], BF16)
            onesbd = self.onesbd
            gate2 = T("ns_gate", [128, 2, 4, 48], F32)
            qaug = T("ns_qaug", [96, 4, 512], BF16)
            ptb = T("ns_pt", [128, NPT, 512], BF16)
            oacc = T("ns_oacc", [128, 4, 4, 64], F32)
            tmpo = T("ns_tmp", [128, 2, 4, 64], F32)
            rd = T("ns_rd", [128, 3, 4], F32)
            coef = T("ns_coef", [128, 3, 4], F32)
            imp = T("ns_imp", [128, 4, 32], F32)
            impt = T("ns_impt", [128, 4, 32], F32)
            m8 = T("ns_m8", [128, 4, 16], F32)
            biasb = T("ns_bias", [128, 4, 96], BF16)
            opair = T("ns_opair", [128, 2, 4, 128], BF16)
            oT = T("ns_oT", [128, NKC, 512], BF16)
            ident = tri[:, 256:384]
            P.op("pool", lambda e: e.dma_start(out=validT[:, :], in_=self.d_valid), writes=["ns_valid"], dma=True)
            P.op("pool", lambda e: e.dma_start(out=tri[:, :], in_=self.d_tri), writes=["ns_tri", "ns_ident"], dma=True)
            P.op("dve", lambda e: e.memset(qaug[:, :, :], 0.0), writes=[("qaug", i) for i in range(4)] + [("qbias", i) for i in range(4)])
            P.op("dve", lambda e: e.memset(biasb[:, :, :], 0.0), writes=["ns_bias"])
            rr = {"pt": 0, "s": 0, "o": 0}
            deferred = []

            def defer(fn, delay):
                deferred.append([delay, fn])

            def step():
                due = []
                for it in deferred:
                    it[0] -= 1
                for it in list(deferred):
                    if it[0] <= 0:
                        due.append(it)
                        deferred.remove(it)
                for it in due:
                    it[1]()

            def flush():
                while deferred:
                    step()

            def next_s():
                b = B_S[rr["s"]]; rr["s"] = (rr["s"] + 1) % 3
                return b

            def next_pt():
                i = rr["pt"]; rr["pt"] = (i + 1) % NPT
                return ptb[:, i, :], ("ns_pt", i)

            slq_by_q = {}

            def q_prologue(Q):
                self.rmsnorm_tile(Q, C_GAIN + 40, xn, 0, "ns_xn", ((sqd2[:, 0, :], sqd2[:, 1, :]), rs2[:, 0, :]), bank=B_A, lnexp=True)
                slg, skg = self.load_w(self.d_wg, 384)
                for sub in range(4):
                    for kc in range(NKC):
                        P.op("pe", lambda e, kc=kc, sub=sub: e.matmul(
                            ps[B_B][:, sub * 48:(sub + 1) * 48], lhsT=xn[:, kc, sub * 128:(sub + 1) * 128],
                            rhs=slg[:, kc * 48:(kc + 1) * 48], start=(kc == 0), stop=(kc == NKC - 1)),
                            reads=[skg, ("ns_xn", kc, 0)], writes=[("ps", B_B)])
                P.op("act", lambda e: e.activation(out=gate2[:, Q % 2, :, :], in_=ps[B_B][:, 0:192].rearrange("p (s c) -> p s c", s=4),
                                                   func=AF.Sigmoid), writes=[("ps", B_B), ("ns_gate", Q % 2)])
                slq_by_q[Q] = [self.load_w(self.d_wq[i], 4096) for i in range(2)]

            q_prologue(0)
            for Q in range(4):
                tq = slice(Q * 512, (Q + 1) * 512)
                gate = gate2[:, Q % 2, :, :]
                gkey = ("ns_gate", Q % 2)
                slq = slq_by_q[Q]
                for g in range(4):
                    for pr in range(2):
                        pi_ = g * 2 + pr
                        slh, skh = slq[pi_ // 4]
                        bq = (B_A, B_B)[pr]
                        bss = next_s()
                        sqk = ("sq", pr)
                        rsk = "rstd" if pr == 0 else "rs1"
                        for kc in range(NKC):
                            o = ((pi_ % 4) * 8 + kc) * 128
                            P.op("pe", lambda e, o=o, kc=kc, slh=slh, bq=bq: e.matmul(
                                ps[bq][:, :], lhsT=slh[:, o:o + 128], rhs=xn[:, kc, :],
                                start=(kc == 0), stop=(kc == NKC - 1)),
                                reads=[skh, ("ns_xn", kc, 0)], writes=[("ps", bq)])
                        P.op("act", lambda e, bq=bq, pr=pr: e.activation(out=sqd2[:, pr, :], in_=ps[bq][:, :], func=AF.Square),
                             writes=[("ps", bq), sqk])
                        P.op("pe", lambda e, bss=bss, pr=pr: e.matmul(ps[bss][:, :], lhsT=onesbd[:, :], rhs=sqd2[:, pr, :], start=True, stop=True),
                             reads=[sqk, "onesbd"], writes=[("ps", bss)])
                        P.op("act", lambda e, bss=bss, pr=pr: e.activation(out=rs2[:, pr, :], in_=ps[bss][:, :], func=AF.Ln,
                                                                         bias=self.gcol(C_EPS), scale=1.0 / 64),
                             reads=["consts"], writes=[("ps", bss), rsk])
                        P.op("act", lambda e, pr=pr: e.activation(out=rs2[:, pr, :], in_=rs2[:, pr, :], func=AF.Exp, scale=-0.5),
                             reads=[rsk], writes=[rsk])
                        for j in range(2):
                            hl = pr * 2 + j
                            pj = slice(j * 64, (j + 1) * 64)
                            P.op("dve", lambda e, bq=bq, pr=pr, hl=hl, pj=pj: e.scalar_tensor_tensor(
                                out=qaug[0:64, hl, :], in0=ps[bq][pj, :], scalar=self.consts[pj, C_DK + 3:C_DK + 4],
                                in1=rs2[pj, pr, :], op0=ALU.mult, op1=ALU.mult),
                                reads=[rsk, "consts"], writes=[("ps", bq), ("qaug", hl)])
                    for hl in range(4):
                        h = g * 4 + hl
                        bS = next_s()
                        pt, ptk = next_pt()
                        bo = (B_CMP, B_SEL, B_WIN)[hl % 3]
                        P.op("pe", lambda e, bS=bS, g=g, hl=hl: e.matmul(
                            ps[bS][0:127, :], lhsT=kcmpT[0:64, g, 0:127], rhs=qaug[0:64, hl, :], start=True, stop=False),
                            reads=["kcmpT", ("qaug", hl)], writes=[("ps", bS)])
                        P.op("pe", lambda e, bS=bS, tq=tq: e.matmul(
                            ps[bS][0:127, :], lhsT=tri[0:127, 256:383], rhs=validT[0:127, tq], start=False, stop=True),
                            reads=["ns_valid", "ns_ident"], writes=[("ps", bS)])
                        P.op("act", lambda e, bS=bS, pt=pt: e.activation(out=pt[0:127, :], in_=ps[bS][0:127, :], func=AF.Exp, scale=0.125),
                             writes=[("ps", bS), ptk])

                        def cmp_tail(hl=hl, h=h, pt=pt, ptk=ptk, bo=bo, g=g, Q=Q, gate=gate, gkey=gkey):
                            for sub in range(4):
                                P.op("pe", lambda e, sub=sub: e.matmul(
                                    ps[bo][:, sub * 97:(sub + 1) * 97], lhsT=pt[0:127, sub * 128:(sub + 1) * 128],
                                    rhs=vcmp[0:127, g, :], start=True, stop=True),
                                    reads=[ptk, ("vcmp_v", g), ("vcmp_o", g), "vcmp_1"], writes=[("ps", bo)])
                            pc = ps[bo][:, 0:388].rearrange("p (s c) -> p s c", s=4)
                            self.branch_coef(pc[:, :, 64:65], rd, coef, gate, h, 0, bo, gkey)
                            P.op("dve", lambda e: e.tensor_tensor(
                                out=oacc[:, hl, :, :], in0=pc[:, :, 0:64], in1=coef[:, 0, :].unsqueeze(2).to_broadcast([128, 4, 64]), op=ALU.mult),
                                reads=["ns_coef"], writes=[("ps", bo), ("ns_oacc", hl)])
                            if Q >= 2:
                                dst = imp if hl == 0 else impt
                                dk_ = "ns_imp" if hl == 0 else "ns_impt"
                                P.op("dve", lambda e: e.tensor_tensor(
                                    out=dst[:, :, :], in0=pc[:, :, 65:97], in1=rd[:, 0, :].unsqueeze(2).to_broadcast([128, 4, 32]), op=ALU.mult),
                                    reads=["ns_rd"], writes=[("ps", bo), dk_])
                                if hl > 0:
                                    P.op("dve", lambda e: e.tensor_tensor(out=imp[:, :, :], in0=imp[:, :, :], in1=impt[:, :, :], op=ALU.add),
                                         reads=["ns_impt"], writes=["ns_imp"])
                        step()
                        defer(cmp_tail, 2)
                    if Q >= 2:
                        flush()
                    if Q >= 2:
                        for sub in range(4):
                            tile = Q * 4 + sub
                            c0 = 31 - 2 * tile
                            P.op("dve", lambda e, sub=sub, c0=c0: e.tensor_tensor(
                                out=imp[:, sub, :], in0=imp[:, sub, :], in1=self.consts[:, C_MUL + c0:C_MUL + c0 + 32], op=ALU.mult),
                                reads=["consts"], writes=["ns_imp"])
                            P.op("dve", lambda e, sub=sub, c0=c0: e.tensor_tensor(
                                out=imp[:, sub, :], in0=imp[:, sub, :], in1=self.consts[:, C_ADD + c0:C_ADD + c0 + 32], op=ALU.add),
                                reads=["consts"], writes=["ns_imp"])
                        P.op("dve", lambda e: e.memset(imp[:, :, 0:1], 3e9), writes=["ns_imp"])
                        for sub in range(4):
                            P.op("dve", lambda e, sub=sub: e.max(out=m8[:, sub, 0:8], in_=imp[:, sub, :]), reads=["ns_imp"], writes=["ns_m8"])
                            P.op("dve", lambda e, sub=sub: e.match_replace(out=impt[:, sub, :], in_to_replace=m8[:, sub, 0:8],
                                                                           in_values=imp[:, sub, :], imm_value=-1e30),
                                 reads=["ns_imp", "ns_m8"], writes=["ns_impt"])
                            P.op("dve", lambda e, sub=sub: e.max(out=m8[:, sub, 8:16], in_=impt[:, sub, :]), reads=["ns_impt"], writes=["ns_m8"])
                            P.op("dve", lambda e, sub=sub: e.tensor_scalar(
                                out=biasb[:, sub, 64:96], in0=imp[:, sub, :], scalar1=m8[:, sub, 15:16], scalar2=-16384.0,
                                op0=ALU.is_lt, op1=ALU.mult), reads=["ns_imp", "ns_m8"], writes=["ns_bias"])

                    def branch(hl, br, first, g=g, Q=Q, gate=gate, gkey=gkey):
                        h = g * 4 + hl
                        kT, kK, vv, vkey = ((kselT, 96, vsel, "vsel"), (kwinT, 64, vwin, "vwin"))[br]
                        bank_o = (B_SEL, B_WIN)[rr["o"]]
                        rr["o"] ^= 1
                        kc_lo = 0 if br == 0 else max(0, 4 * Q - 4)
                        kc_hi = 4 * Q + 3
                        for kc in range(kc_lo, kc_hi + 1):
                            d = kc - 4 * Q
                            lo = max(0, d)
                            hi = 3 if br == 0 else min(3, d + 4)
                            f0, f1 = lo * 128, (hi + 1) * 128
                            bS = next_s()
                            pt, ptk = next_pt()
                            kk = [("kselT", g, kc // 4), ("kselE", g)] if br == 0 else [("kwinT", g, kc // 4)]
                            qk = [("qaug", hl), ("qbias", hl)] if br == 0 else [("qaug", hl)]
                            mblk = None
                            if d >= 0:
                                mblk = (d, 0)
                            elif br == 1 and d + 4 <= 3:
                                mblk = (d + 4, 128)
                            P.op("pe", lambda e, bS=bS, kc=kc, f0=f0, f1=f1, mblk=mblk: e.matmul(
                                ps[bS][:, f0:f1], lhsT=kT[0:kK, g, kc * 128:(kc + 1) * 128], rhs=qaug[0:kK, hl, f0:f1],
                                start=True, stop=(mblk is None)), reads=kk + qk, writes=[("ps", bS)])
                            if mblk is not None:
                                P.op("pe", lambda e, bS=bS, mblk=mblk: e.matmul(
                                    ps[bS][:, mblk[0] * 128:(mblk[0] + 1) * 128], lhsT=ident, rhs=tri[:, mblk[1]:mblk[1] + 128],
                                    start=False, stop=True), reads=["ns_tri", "ns_ident"], writes=[("ps", bS)])
                            P.op("act", lambda e, bS=bS, pt=pt, f0=f0, f1=f1: e.activation(
                                out=pt[:, f0:f1], in_=ps[bS][:, f0:f1], func=AF.Exp, scale=0.125),
                                writes=[("ps", bS), ptk])

                            def tail(lo=lo, hi=hi, pt=pt, ptk=ptk, kc=kc, last=(kc == kc_hi)):
                                for sub in range(lo, hi + 1):
                                    P.op("pe", lambda e, sub=sub: e.matmul(
                                        ps[bank_o][:, sub * 65:(sub + 1) * 65], lhsT=pt[:, sub * 128:(sub + 1) * 128],
                                        rhs=vv[:, kc, g, :], start=(kc == kc_lo and sub == lo), stop=(kc == 4 * Q + sub), skip_group_check=True),
                                        reads=[ptk, (vkey, kc), vkey + "_1"], writes=[("ps", bank_o)])
                                if not last:
                                    return
                                po = ps[bank_o][:, 0:260].rearrange("p (s c) -> p s c", s=4)
                                self.branch_coef(po[:, :, 64:65], rd, coef, gate, h, br + 1, bank_o, gkey)
                                tmk = ("ns_tmpo", br)
                                P.op("dve", lambda e: e.tensor_tensor(
                                    out=tmpo[:, br, :, :], in0=po[:, :, 0:64], in1=coef[:, br + 1, :].unsqueeze(2).to_broadcast([128, 4, 64]), op=ALU.mult),
                                    reads=["ns_coef"], writes=[("ps", bank_o), tmk])
                                if first:
                                    P.op("dve", lambda e: e.tensor_tensor(out=oacc[:, hl, :, :], in0=oacc[:, hl, :, :], in1=tmpo[:, br, :, :], op=ALU.add),
                                         reads=[tmk], writes=[("ns_oacc", hl)])
                                    return
                                pi = (h // 2) % 2
                                P.op("dve", lambda e: e.tensor_tensor(
                                    out=opair[:, pi, :, (h % 2) * 64:(h % 2) * 64 + 64], in0=oacc[:, hl, :, :], in1=tmpo[:, br, :, :], op=ALU.add),
                                    reads=[tmk, ("ns_oacc", hl)], writes=[("ns_opair", pi, h % 2)])
                                if h % 2 == 1:
                                    def tr(c=h // 2, pi=pi):
                                        pb2 = ps[B_B][:, :].bitcast(BF16)
                                        for sub in range(4):
                                            P.op("pe", lambda e, sub=sub: e.transpose(
                                                out=pb2[:, sub * 128:(sub + 1) * 128], in_=opair[:, pi, sub, :], identity=ident),
                                                reads=[("ns_opair", pi, 0), ("ns_opair", pi, 1), "ns_ident"], writes=[("ps", B_B)])
                                        P.op("act", lambda e: e.activation(out=oT[:, c, :], in_=pb2[:, 0:512], func=AF.Copy),
                                             writes=[("ps", B_B), ("ns_oT", c)])
                                    defer(tr, 6)
                            step()
                            defer(tail, 2)

                    for hl in range(4):
                        branch(hl, 1, True)
                    if g == 3 and Q < 3:
                        q_prologue(Q + 1)
                    if Q >= 2:
                        pb = ps[B_A][:, :].bitcast(BF16)
                        for sub in range(4):
                            P.op("pe", lambda e, sub=sub, pb=pb: e.transpose(out=pb[0:96, sub * 128:(sub + 1) * 128], in_=biasb[:, sub, :], identity=ident),
                                 reads=["ns_bias", "ns_ident"], writes=[("ps", B_A)])
                        for hl in range(4):
                            if hl % 2 == 0:
                                P.op("act", lambda e, hl=hl, pb=pb: e.activation(out=qaug[64:96, hl, :], in_=pb[64:96, 0:512], func=AF.Copy),
                                     writes=[("ps", B_A), ("qbias", hl)])
                            else:
                                P.op("dve", lambda e, hl=hl, pb=pb: e.tensor_copy(out=qaug[64:96, hl, :], in_=pb[64:96, 0:512]),
                                     writes=[("ps", B_A), ("qbias", hl)])
                    for hl in range(4):
                        branch(hl, 0, False)
                flush()
                slo = [self.load_w(self.d_wo[i], 4096) for i in range(2)]
                for mo in range(NKC):
                    sl, sk = slo[mo // 4]
                    b = B_A if mo % 2 == 0 else B_B
                    for kc in range(NKC):
                        o = ((mo % 4) * 8 + kc) * 128
                        P.op("pe", lambda e, b=b, o=o, kc=kc, sl=sl: e.matmul(
                            ps[b][:, :], lhsT=sl[:, o:o + 128], rhs=oT[:, kc, :], start=(kc == 0), stop=(kc == NKC - 1)),
                            reads=[sk, ("ns_oT", kc)], writes=[("ps", b)])
                    P.op("dve", lambda e, b=b, mo=mo, tq=tq: e.tensor_tensor(
                        out=self.hT[:, mo, tq], in0=ps[b][:, :], in1=self.hT[:, mo, tq], op=ALU.add),
                        writes=[("ps", b), ("hT", mo, Q)])

    def branch_coef(self, den, rd, coef, gate, h, br, bank, gkey="ns_gate"):
        P = self.P
        rk = ("ns_rd", br)
        P.op("dve", lambda e: e.tensor_scalar(out=rd[:, br, :].unsqueeze(2), in0=den, scalar1=self.gcol(C_TINY), scalar2=None, op0=ALU.add),
             reads=["consts"], writes=[("ps", bank), rk, "ns_rd"])
        P.op("dve", lambda e: e.reciprocal(out=rd[:, br, :], in_=rd[:, br, :]), reads=[rk], writes=[rk, "ns_rd"])
        P.op("dve", lambda e: e.tensor_tensor(out=coef[:, br, :], in0=rd[:, br, :], in1=gate[:, :, 3 * h + br], op=ALU.mult),
             reads=[rk, gkey], writes=["ns_coef"])

    def build(self):
        nc, P = self.nc, self.P
        self.prologue()
        for ph in self.phases:
            if ph.startswith("ffn"):
                self.ffn(int(ph[3:]))
            elif ph == "conv":
                self.conv()
            elif ph == "kv":
                self.kv()
            elif ph == "nsa":
                self.nsa()
        self.epilogue()
        with nc.Block() as block:
            sems = {e: nc.alloc_semaphore(f"s_{e}") for e in Prog.ENGS}
            dsems = {q: [nc.alloc_semaphore(f"d{q}_{i}") for i in range(Prog.N_DMA_SEMS)] for q in ("pool", "sp")}
            P.emit(block, sems, dsems)
        return nc


def _prep_inputs(inputs, phases):
    f = lambda a: np.ascontiguousarray(np.asarray(a, np.float32))
    m = {}
    consts = np.zeros((128, NCONST), np.float32)
    consts[:, 0:32] = _fm(f(inputs["ffn_norm"]).reshape(4, D))
    consts[:, 32:48] = _fm(f(inputs["mix_norm"]))
    consts[:, 48:56] = _fm(f(inputs["kv_norm"]).reshape(1, D))
    consts[:, C_CONVW:C_CONVW + 24] = _fm(f(inputs["conv_w"]).reshape(3, D))
    kn = f(inputs["k_norm"])
    consts[0:64, C_DK:C_DK + 3] = kn.T
    consts[64:128, C_DK:C_DK + 3] = kn.T
    consts[0:64, C_DK + 3] = f(inputs["q_norm"])[0]
    consts[64:128, C_DK + 3] = f(inputs["q_norm"])[0]
    b1 = f(inputs["cmp_b1"])
    consts[:, C_B1:C_B1 + 4] = b1.reshape(2, 2, 128).transpose(2, 0, 1).reshape(128, 4)
    consts[:, C_EPS] = EPS
    m["consts"] = consts
    gu = f(inputs["ffn_w_gate_up"]).reshape(4, D, 2 * DFF)
    dn = f(inputs["ffn_w_down"]).reshape(4, DFF, D)
    m["w_gu"] = np.stack([_img_gu(gu[i]) for i in range(4)])
    m["w_dn"] = np.stack([_img_dn(dn[i]) for i in range(4)])
    m["w_cin"] = _img_cin(f(inputs["conv_w_in"])[0])
    m["w_cout"] = _img_sq(f(inputs["conv_w_out"])[0])
    kvw = f(inputs["kv_w"]).reshape(8, 128, 6, 4, 64)
    ra = kvw[:, :, [0, 1]]
    ra = ra.transpose(1, 3, 0, 2, 4).reshape(128, 4096)
    kb = kvw[:, :, [2, 4]]
    kb = kb.transpose(1, 3, 0, 2, 4).reshape(128, 4096)
    m["w_kvk"] = np.ascontiguousarray(np.stack([ra, kb]))
    vv = kvw[:, :, [3, 5]]
    m["w_kvv"] = np.ascontiguousarray(vv.transpose(1, 0, 2, 3, 4)).reshape(128, 4096)
    w1 = f(inputs["cmp_w1"]).reshape(2, 2, 16, 64, 256)
    m["w_c1"] = np.ascontiguousarray(w1.transpose(0, 1, 3, 2, 4)).reshape(2, 2, 64, 4096)
    w2 = f(inputs["cmp_w2"]).reshape(2, 2, 128, 64)
    m["w_c2"] = np.ascontiguousarray(w2.transpose(2, 0, 1, 3)).reshape(128, 256)
    pos = f(inputs["cmp_pos"])
    m["posT"] = np.ascontiguousarray(pos.transpose(0, 2, 1)).reshape(128, 32)
    wqg = f(inputs["nsa_w_qg"])[0]
    m["w_q"] = _img_sq(np.ascontiguousarray(wqg[:, :1024]))
    wg = wqg[:, 1024:].reshape(8, 128, 48)
    m["w_g"] = np.ascontiguousarray(wg.transpose(1, 0, 2)).reshape(128, 384)
    m["w_o"] = _img_sq(f(inputs["nsa_w_o"])[0])
    m.update(_mask_consts())
    m["consts"][:, C_MUL:C_MUL + 63] = m.pop("_mul")
    m["consts"][:, C_ADD:C_ADD + 63] = m.pop("_add")
    m["consts"][:, C_TINY] = 1e-30
    return m


_MASKS = {}


def _mask_consts():
    if _MASKS:
        return dict(_MASKS)
    key = np.arange(S)
    eind = (key[None, :] // 64 == np.arange(32)[:, None]).astype(np.float32)
    n = np.arange(128)
    valid = np.where((16 * n[:, None] + 31 <= key[None, :]) & (n[:, None] < 127), 0.0, -16384.0).astype(np.float32)
    p = np.arange(128)
    tri = np.concatenate([np.where(p[:, None] > p[None, :], -16384.0, 0.0), np.where(p[:, None] <= p[None, :], -16384.0, 0.0),
                          (p[:, None] == p[None, :]).astype(np.float64)], axis=1).astype(np.float32)
    j = np.arange(32)
    ovl = ((16 * n[:, None] < 64 * j[None, :] + 64) & (16 * n[:, None] + 32 > 64 * j[None, :]) & (n[:, None] < 127)).astype(np.float32)
    c = (p >= 64).astype(np.int64)[:, None]
    jr = (np.arange(63) - 31)[None, :]
    forced0 = jr == c
    forced1 = jr == c - 1
    future = jr > c
    mul = (~(forced0 | forced1 | future)).astype(np.float32)
    add = np.where(forced0, 2e9, np.where(forced1, 1e9, np.where(future, -1e30, 0.0))).astype(np.float32)
    _MASKS.update({"m_eind": eind, "m_valid": valid, "m_tri": tri, "m_ovl": ovl, "_mul": mul, "_add": add})
    return dict(_MASKS)


_NC_CACHE = {}
_RUN_KW = {}
_LAST = {}


def run_phases(inputs, phases, xT_list):
    key = tuple(phases)
    if key not in _NC_CACHE:
        _NC_CACHE[key] = Builder(phases).build()
    nc = _NC_CACHE[key]
    shared = _prep_inputs(inputs, phases)
    in_maps = []
    for xT in xT_list:
        d = dict(shared)
        d["xT"] = np.ascontiguousarray(xT, np.float32)
        in_maps.append(d)
    res = run_bass_kernel_spmd(nc, in_maps, core_ids=list(range(len(xT_list))), **_RUN_KW)
    _LAST["res"] = res
    return [r["yT"] for r in res.results]


def kernel(**inputs):
    x = np.asarray(inputs["x"], np.float32)
    xT = [np.ascontiguousarray(x[b].T) for b in range(x.shape[0])]
    yT = run_phases(inputs, ALL_PHASES, xT)
    return np.stack([np.ascontiguousarray(y.T) for y in yT]).astype(np.float32)
```
